# Optimizing a Trainium2 kernel written in Bass

```python
import jax
import jax.numpy as jnp
from jax import lax
import numpy as np

D_MODEL = 1024
BATCH = 8
SEQ = 8192
DEPTH = 2

GRID_W = 64
CTX_LEN = 256
N_EVEN = (DEPTH + 1) // 2
N_ODD = DEPTH // 2

HEAD_DIM = 64
ATTN_HEADS = (D_MODEL // 2) // HEAD_DIM
ATTN_KV_HEADS = ATTN_HEADS // 4
ATTN_GROUP = ATTN_HEADS // ATTN_KV_HEADS
WINDOW = 128
ATTN_BLOCK = 128
ROPE_BASE = 10000.0

HGRN_EXPAND = 128
HGRN_WIDTH = D_MODEL // 2
HGRN_HEADS = HGRN_WIDTH // HGRN_EXPAND
HGRN_DK = HGRN_EXPAND
HGRN_DV = HGRN_EXPAND
CHUNK = 64

RET_HEADS = 4
RET_DK = D_MODEL // RET_HEADS
RET_DV = 2 * RET_DK
RET_BASE = 10000.0

FFN_HIDDEN = -(-8 * D_MODEL // (3 * 256)) * 256

A_Q = ATTN_HEADS * HEAD_DIM
A_KV = ATTN_KV_HEADS * HEAD_DIM
EVEN_PARTS = (('a_q', A_Q), ('a_k', A_KV), ('a_v', A_KV), ('b_q', HGRN_WIDTH), ('b_ff', HGRN_WIDTH),
              ('b_fb', HGRN_WIDTH), ('b_i', HGRN_WIDTH), ('b_g', HGRN_WIDTH))
EVEN_IN = A_Q + 2 * A_KV + 5 * HGRN_WIDTH
EVEN_OUT = A_Q + HGRN_HEADS * HGRN_DV
ODD_PARTS = (('q', RET_HEADS * RET_DK), ('k', RET_HEADS * RET_DK), ('v', RET_HEADS * RET_DV),
             ('g', RET_HEADS * RET_DV))
ODD_IN = 2 * RET_HEADS * RET_DK + 2 * RET_HEADS * RET_DV
ODD_OUT = RET_HEADS * RET_DV
EPS = 1e-6
F32 = jnp.float32

kernel_name = 'hybrid_swa_hgrn2_retention_dit_prefix'


def _offsets(parts):
    out, start = {}, 0
    for name, width in parts:
        out[name] = (start, start + width)
        start += width
    return out


def split_parts(u, parts):
    return {n: u[..., s:e] for n, (s, e) in _offsets(parts).items()}


def project_parts(h, w, parts, names):
    off = _offsets(parts)
    return {n: h @ w[:, off[n][0]:off[n][1]] for n in names}


def rms_norm(x, g=None):
    xf = x.astype(F32)
    y = xf * lax.rsqrt(jnp.mean(xf * xf, axis=-1, keepdims=True) + EPS)
    if g is not None:
        y = y * g.astype(F32)
    return y.astype(x.dtype)


def modulate(h, shift, scale):
    return h * (1.0 + scale) + shift


def to_heads(u, n_heads):
    b, l, _ = u.shape
    return u.reshape(b, l, n_heads, -1).transpose(0, 2, 1, 3)


def from_heads(o):
    return o.transpose(0, 2, 1, 3)


def flip(a):
    return jnp.flip(a, axis=2)


def axial_rope_tables(n_tok):
    t = jnp.arange(n_tok)
    row = (t // GRID_W).astype(F32)
    col = (t % GRID_W).astype(F32)
    n_freq = HEAD_DIM // 4
    inv = ROPE_BASE ** (-jnp.arange(n_freq, dtype=F32) / n_freq)
    ang = jnp.concatenate([row[:, None] * inv, col[:, None] * inv], axis=-1)
    return jnp.cos(ang), jnp.sin(ang)


def retention_rope_tables(n_tok):
    theta = 1.0 / (RET_BASE ** jnp.linspace(0.0, 1.0, RET_DK // 2, dtype=F32))
    ang = jnp.arange(n_tok, dtype=F32)[:, None] * theta
    return jnp.cos(ang), jnp.sin(ang)


def apply_rope(x, cos, sin):
    half = x.shape[-1] // 2
    xf = x.astype(F32)
    x1, x2 = xf[..., :half], xf[..., half:]
    return jnp.concatenate([x1 * cos - x2 * sin, x2 * cos + x1 * sin], axis=-1).astype(x.dtype)


def attn_q_heads(u, g):
    b, l, _ = u.shape
    q = rms_norm(u.reshape(b, l, ATTN_KV_HEADS, ATTN_GROUP, HEAD_DIM), g)
    return q.transpose(0, 2, 3, 1, 4)


def kv_heads(u, g=None):
    b, l, _ = u.shape
    h = u.reshape(b, l, ATTN_KV_HEADS, HEAD_DIM)
    if g is not None:
        h = rms_norm(h, g)
    return h.transpose(0, 2, 1, 3)


def window_attention(q, k, v, k_ctx, v_ctx, sink):
    b, kv, g, l, d = q.shape
    n_blocks = l // ATTN_BLOCK
    span = ATTN_BLOCK + 2 * WINDOW
    pad = ((0, 0), (0, 0), (WINDOW, WINDOW), (0, 0))
    kp, vp = jnp.pad(k, pad), jnp.pad(v, pad)
    scale = d ** -0.5
    sink_col = jnp.broadcast_to(sink[None, :, :, None, None], (b, kv, g, ATTN_BLOCK, 1))

    def block(i):
        start = i * ATTN_BLOCK
        qb = lax.dynamic_slice_in_dim(q, start, ATTN_BLOCK, axis=3)
        kb = lax.dynamic_slice_in_dim(kp, start, span, axis=2)
        vb = lax.dynamic_slice_in_dim(vp, start, span, axis=2)
        q_pos = start + jnp.arange(ATTN_BLOCK)
        k_pos = start - WINDOW + jnp.arange(span)
        valid = ((jnp.abs(k_pos[None, :] - q_pos[:, None]) <= WINDOW)
                 & (k_pos >= 0)[None, :] & (k_pos < l)[None, :])
        s_lat = jnp.einsum('bkgqd,bksd->bkgqs', qb, kb, preferred_element_type=F32) * scale
        s_lat = jnp.where(valid, s_lat, -jnp.inf)
        s_ctx = jnp.einsum('bkgqd,bkcd->bkgqc', qb, k_ctx, preferred_element_type=F32) * scale
        p = jax.nn.softmax(jnp.concatenate([sink_col, s_lat, s_ctx], axis=-1), axis=-1)
        p_lat = p[..., 1:1 + span].astype(v.dtype)
        p_ctx = p[..., 1 + span:].astype(v.dtype)
        return (jnp.einsum('bkgqs,bksd->bkgqd', p_lat, vb)
                + jnp.einsum('bkgqc,bkcd->bkgqd', p_ctx, v_ctx))

    o = lax.map(block, jnp.arange(n_blocks))
    return o.transpose(1, 0, 4, 2, 3, 5).reshape(b, l, kv * g * d)


def context_attention(q, k, v, sink):
    b, kv, g, lc, d = q.shape
    s = jnp.einsum('bkgqd,bkcd->bkgqc', q, k, preferred_element_type=F32) * d ** -0.5
    sink_col = jnp.broadcast_to(sink[None, :, :, None, None], (b, kv, g, lc, 1))
    p = jax.nn.softmax(jnp.concatenate([sink_col, s], axis=-1), axis=-1)[..., 1:].astype(v.dtype)
    o = jnp.einsum('bkgqc,bkcd->bkgqd', p, v)
    return o.transpose(0, 3, 1, 2, 4).reshape(b, lc, kv * g * d)


def chunk_scan(q_in, k_in, v, decay, s0):
    xs = (jnp.moveaxis(q_in, 2, 0), jnp.moveaxis(k_in, 2, 0), jnp.moveaxis(v, 2, 0), decay)

    def step(s, inp):
        q_n, k_n, v_n, a_n = inp
        o_n = jnp.einsum('bhcd,bhde->bhce', q_n, s)
        s = a_n * s + jnp.einsum('bhcd,bhce->bhde', k_n, v_n)
        return s, o_n

    s, o = lax.scan(step, s0, xs)
    return jnp.moveaxis(o, 0, 2), s


def gla_chunked(q, k, v, log_f, s0):
    b, h, l, dk = q.shape
    dv = v.shape[-1]
    n = l // CHUNK
    qc, kc, lc = (a.reshape(b, h, n, CHUNK, dk) for a in (q, k, log_f))
    vc = v.reshape(b, h, n, CHUNK, dv)
    cum = jnp.cumsum(lc, axis=3)
    ref = cum[:, :, :, CHUNK // 2:CHUNK // 2 + 1]
    scores = jnp.einsum('bhntd,bhnsd->bhnts', qc * jnp.exp(cum - ref), kc * jnp.exp(ref - cum))
    lower = jnp.tril(jnp.ones((CHUNK, CHUNK), dtype=bool))
    o_intra = jnp.einsum('bhnts,bhnse->bhnte', jnp.where(lower, scores, 0.0), vc)
    cum_last = cum[:, :, :, -1:]
    decay = jnp.moveaxis(jnp.exp(cum_last[:, :, :, 0]), 2, 0)[..., None]
    o_inter, s = chunk_scan(qc * jnp.exp(cum), kc * jnp.exp(cum_last - cum), vc, decay, s0)
    return (o_intra + o_inter).reshape(b, h, l, dv), s


def gla_final_state(k, v, log_f):
    cum = jnp.cumsum(log_f, axis=2)
    return jnp.einsum('bhld,bhle->bhde', k * jnp.exp(cum[:, :, -1:] - cum), v)


def retention_chunked(q, k, v, log_gamma, s0):
    b, h, l, dk = q.shape
    dv = v.shape[-1]
    n = l // CHUNK
    qc = q.reshape(b, h, n, CHUNK, dk)
    kc = k.reshape(b, h, n, CHUNK, dk)
    vc = v.reshape(b, h, n, CHUNK, dv)
    pos = jnp.arange(CHUNK, dtype=F32)
    rel = pos[:, None] - pos[None, :]
    dmat = jnp.where(rel >= 0, jnp.exp(log_gamma[:, None, None] * jnp.maximum(rel, 0.0)), 0.0)
    scores = jnp.einsum('bhntd,bhnsd->bhnts', qc, kc) * dmat[None, :, None]
    o_intra = jnp.einsum('bhnts,bhnse->bhnte', scores, vc)
    lg = log_gamma[:, None]
    q_in = qc * jnp.exp(lg * (pos + 1.0))[None, :, None, :, None]
    k_in = kc * jnp.exp(lg * (CHUNK - 1.0 - pos))[None, :, None, :, None]
    decay = jnp.broadcast_to(jnp.exp(log_gamma * CHUNK)[None, None, :, None, None], (n, 1, h, 1, 1))
    o_inter, s = chunk_scan(q_in, k_in, vc, decay, s0)
    return (o_intra + o_inter).reshape(b, h, l, dv), s


def retention_final_state(k, v, log_gamma):
    lc = k.shape[2]
    w = jnp.exp(log_gamma[:, None] * (lc - 1.0 - jnp.arange(lc, dtype=F32)))
    return jnp.einsum('bhld,bhle->bhde', k * w[None, :, :, None], v)


def gated_head_norm(o, g_raw, gain=None):
    b, h, l, dv = o.shape
    y = rms_norm(from_heads(o), gain) * jax.nn.silu(g_raw.reshape(b, l, h, dv).astype(F32))
    return y.reshape(b, l, h * dv)


def even_mixer(h_ctx, h_lat, w_in, w_out, qk_g, sink, out_g, lb, cos, sin, need_ctx):
    dt = h_lat.dtype
    n_b = h_ctx.shape[0]
    sink = sink.astype(F32).reshape(ATTN_KV_HEADS, ATTN_GROUP)
    lb = lb.reshape(HGRN_HEADS, 1, HGRN_DK)
    p = split_parts(h_lat @ w_in, EVEN_PARTS)
    names = [n for n, _ in EVEN_PARTS] if need_ctx else ['a_k', 'a_v', 'b_ff', 'b_fb', 'b_i']
    pc = project_parts(h_ctx, w_in, EVEN_PARTS, names)

    def gates(f_raw):
        f = lb + (1.0 - lb) * jax.nn.sigmoid(to_heads(f_raw, HGRN_HEADS).astype(F32))
        return 1.0 - f, jnp.log(f)

    k_ctx = kv_heads(pc['a_k'], qk_g[1])
    v_ctx = kv_heads(pc['a_v'])
    q_lat = apply_rope(attn_q_heads(p['a_q'], qk_g[0]), cos, sin)
    k_lat = apply_rope(kv_heads(p['a_k'], qk_g[1]), cos, sin)
    a_lat = window_attention(q_lat, k_lat, kv_heads(p['a_v']), k_ctx, v_ctx, sink)

    k_fw_c, lf_fw_c = gates(pc['b_ff'])
    k_bw_c, lf_bw_c = gates(pc['b_fb'])
    i_c = to_heads(pc['b_i'], HGRN_HEADS).astype(F32)
    if need_ctx:
        zeros = jnp.zeros((n_b, HGRN_HEADS, HGRN_DK, HGRN_DV), F32)
        q_c = jax.nn.silu(to_heads(pc['b_q'], HGRN_HEADS).astype(F32))
        o_fw_c, s_fw = gla_chunked(q_c, k_fw_c, i_c, lf_fw_c, zeros)
        o_bw_c, s_bw = gla_chunked(flip(q_c), flip(k_bw_c), flip(i_c), flip(lf_bw_c), zeros)
    else:
        s_fw = gla_final_state(k_fw_c, i_c, lf_fw_c)
        s_bw = gla_final_state(flip(k_bw_c), flip(i_c), flip(lf_bw_c))
    k_fw, lf_fw = gates(p['b_ff'])
    k_bw, lf_bw = gates(p['b_fb'])
    q_l = jax.nn.silu(to_heads(p['b_q'], HGRN_HEADS).astype(F32))
    i_l = to_heads(p['b_i'], HGRN_HEADS).astype(F32)
    o_fw, _ = gla_chunked(q_l, k_fw, i_l, lf_fw, s_fw)
    o_bw, _ = gla_chunked(flip(q_l), flip(k_bw), flip(i_l), flip(lf_bw), s_bw)
    b_lat = gated_head_norm(o_fw + flip(o_bw), p['b_g'], out_g)

    y_lat = jnp.concatenate([a_lat.astype(dt), b_lat.astype(dt)], axis=-1) @ w_out
    if not need_ctx:
        return None, y_lat
    a_ctx = context_attention(attn_q_heads(pc['a_q'], qk_g[0]), k_ctx, v_ctx, sink)
    b_ctx = gated_head_norm(o_fw_c + flip(o_bw_c), pc['b_g'], out_g)
    y_ctx = jnp.concatenate([a_ctx.astype(dt), b_ctx.astype(dt)], axis=-1) @ w_out
    return y_ctx, y_lat


def odd_mixer(h_ctx, h_lat, w_in, w_out, cos, sin, need_ctx):
    dt = h_lat.dtype
    n_b = h_ctx.shape[0]
    log_g_fw = jnp.log(1.0 - 2.0 ** (-5.0 - jnp.arange(RET_HEADS, dtype=F32)))
    log_g_bw = log_g_fw[::-1]
    k_scale = RET_DK ** -0.5
    p = split_parts(h_lat @ w_in, ODD_PARTS)
    names = [n for n, _ in ODD_PARTS] if need_ctx else ['k', 'v']
    pc = project_parts(h_ctx, w_in, ODD_PARTS, names)

    k_c = to_heads(pc['k'], RET_HEADS).astype(F32) * k_scale
    v_c = to_heads(pc['v'], RET_HEADS).astype(F32)
    if need_ctx:
        zeros = jnp.zeros((n_b, RET_HEADS, RET_DK, RET_DV), F32)
        q_c = to_heads(pc['q'], RET_HEADS).astype(F32)
        o_fw_c, s_fw = retention_chunked(q_c, k_c, v_c, log_g_fw, zeros)
        o_bw_c, s_bw = retention_chunked(flip(q_c), flip(k_c), flip(v_c), log_g_bw, zeros)
    else:
        s_fw = retention_final_state(k_c, v_c, log_g_fw)
        s_bw = retention_final_state(flip(k_c), flip(v_c), log_g_bw)

    q_l = apply_rope(to_heads(p['q'], RET_HEADS).astype(F32), cos, sin)
    k_l = apply_rope(to_heads(p['k'], RET_HEADS).astype(F32), cos, sin) * k_scale
    v_l = to_heads(p['v'], RET_HEADS).astype(F32)
    o_fw, _ = retention_chunked(q_l, k_l, v_l, log_g_fw, s_fw)
    o_bw, _ = retention_chunked(flip(q_l), flip(k_l), flip(v_l), log_g_bw, s_bw)
    y_lat = gated_head_norm(o_fw + flip(o_bw), p['g']).astype(dt) @ w_out
    if not need_ctx:
        return None, y_lat
    y_ctx = gated_head_norm(o_fw_c + flip(o_bw_c), pc['g']).astype(dt) @ w_out
    return y_ctx, y_lat


def swiglu(h, w_in, w_out):
    gate, up = jnp.split(h @ w_in, 2, axis=-1)
    return (jax.nn.silu(gate) * up) @ w_out


def setup_inputs(seed: int = 0) -> dict:
    key = jax.random.key(seed)
    ks = jax.random.split(key, 17)
    d = D_MODEL

    def nrm(k, shape, scale):
        return jax.random.normal(k, shape, F32) * scale

    return {
        'x': nrm(ks[0], (BATCH, SEQ, d), 1.0),
        'c': nrm(ks[1], (BATCH, d), 1.0),
        'ctx': nrm(ks[2], (BATCH, CTX_LEN, d), 1.0),
        'c_ctx': nrm(ks[3], (d,), 1.0),
        'mod_w': nrm(ks[4], (DEPTH, d, 6 * d), 0.5 * d ** -0.5),
        'mod_b': nrm(ks[5], (DEPTH, 6 * d), 0.02),
        'norm_g': 1.0 + nrm(ks[6], (DEPTH, 2, d), 0.02),
        'ffn_w_in': nrm(ks[7], (DEPTH, d, 2 * FFN_HIDDEN), d ** -0.5),
        'ffn_w_out': nrm(ks[8], (DEPTH, FFN_HIDDEN, d), FFN_HIDDEN ** -0.5),
        'even_w_in': nrm(ks[9], (N_EVEN, d, EVEN_IN), d ** -0.5),
        'even_w_out': nrm(ks[10], (N_EVEN, EVEN_OUT, d), EVEN_OUT ** -0.5),
        'attn_qk_norm_g': 1.0 + nrm(ks[11], (N_EVEN, 2, HEAD_DIM), 0.02),
        'attn_sink': nrm(ks[12], (N_EVEN, ATTN_HEADS), 0.5),
        'hgrn_out_norm_g': 1.0 + nrm(ks[13], (N_EVEN, HGRN_DV), 0.02),
        'hgrn_lb': nrm(ks[14], (N_EVEN + 1, HGRN_WIDTH), 0.1),
        'odd_w_in': nrm(ks[15], (N_ODD, d, ODD_IN), d ** -0.5),
        'odd_w_out': nrm(ks[16], (N_ODD, ODD_OUT, d), ODD_OUT ** -0.5),
    }


def reference(x, c, ctx, c_ctx, mod_w, mod_b, norm_g, ffn_w_in, ffn_w_out, even_w_in, even_w_out,
              attn_qk_norm_g, attn_sink, hgrn_out_norm_g, hgrn_lb, odd_w_in, odd_w_out):
    n_tok = x.shape[1]
    rope_cos, rope_sin = axial_rope_tables(n_tok)
    ret_cos, ret_sin = retention_rope_tables(n_tok)
    lower_bounds = jnp.cumsum(jax.nn.softmax(hgrn_lb.astype(F32), axis=0), axis=0)
    cond_lat = jax.nn.silu(c)
    cond_ctx = jax.nn.silu(c_ctx)
    for layer in range(DEPTH):
        last = layer == DEPTH - 1
        j = layer // 2
        m_lat = jnp.split((cond_lat @ mod_w[layer] + mod_b[layer])[:, None, :], 6, axis=-1)
        m_ctx = jnp.split((cond_ctx @ mod_w[layer] + mod_b[layer])[None, None, :], 6, axis=-1)
        h_lat = modulate(rms_norm(x, norm_g[layer, 0]), m_lat[0], m_lat[1])
        h_ctx = modulate(rms_norm(ctx, norm_g[layer, 0]), m_ctx[0], m_ctx[1])
        if layer % 2 == 0:
            y_ctx, y_lat = even_mixer(h_ctx, h_lat, even_w_in[j], even_w_out[j], attn_qk_norm_g[j],
                                      attn_sink[j], hgrn_out_norm_g[j], lower_bounds[j],
                                      rope_cos, rope_sin, not last)
        else:
            y_ctx, y_lat = odd_mixer(h_ctx, h_lat, odd_w_in[j], odd_w_out[j], ret_cos, ret_sin, not last)
        x = x + m_lat[2] * y_lat
        x = x + m_lat[5] * swiglu(modulate(rms_norm(x, norm_g[layer, 1]), m_lat[3], m_lat[4]),
                                  ffn_w_in[layer], ffn_w_out[layer])
        if not last:
            ctx = ctx + m_ctx[2] * y_ctx
            ctx = ctx + m_ctx[5] * swiglu(modulate(rms_norm(ctx, norm_g[layer, 1]), m_ctx[3], m_ctx[4]),
                                          ffn_w_in[layer], ffn_w_out[layer])
    return x
```

```python
import numpy as np
from contextlib import ExitStack
import concourse.bass as bass
import concourse.mybir as mybir
from concourse.bass_utils import run_bass_kernel_spmd

F32 = mybir.dt.float32
BF16 = mybir.dt.bfloat16
AF = mybir.ActivationFunctionType
ALU = mybir.AluOpType
AX = mybir.AxisListType

D = 1024
KC = 8
L = 8192
LC = 256
T = L + LC
NB = T // 128
TILES = [(0, 256)] + [(256 + 512 * i, 512) for i in range(16)]
FH = 2816
EPS = 1e-6


class Dep:
    __slots__ = ("w", "r")

    def __init__(self):
        self.w = {}
        self.r = {}


class Eng:
    def __init__(self, name, e, sem):
        self.name, self.e, self.sem = name, e, sem
        self.count = 0
        self.seen = {}
        self.pending = False


class DSem:
    def __init__(self, name, sem):
        self.name, self.sem = name, sem
        self.count = 0


class MK:
    NDMA = 32

    def __init__(self, nc, es):
        self.nc = nc
        self.E = {}
        for n, a in [("pe", "tensor"), ("act", "scalar"), ("dve", "vector"), ("pool", "gpsimd"), ("sp", "sync")]:
            self.E[n] = Eng(n, getattr(nc, a), es.enter_context(nc.semaphore("s_" + n)))
        self.semof = {n: e.sem for n, e in self.E.items()}
        self.dsems = []
        for i in range(self.NDMA):
            d = DSem("d%d" % i, es.enter_context(nc.semaphore("s_d%d" % i)))
            self.dsems.append(d)
            self.semof[d.name] = d.sem
        self.dnext = 0
        self.psum = []
        self.psn = 0
        self.ninstr = 0

    def _wait(self, eng, key, val):
        if val <= 0 or eng.seen.get(key, 0) >= val:
            return
        eng.e.wait_ge(self.semof[key], val)
        eng.seen[key] = val
        self.ninstr += 1

    def _deps(self, eng, reads, writes):
        need = {}
        for b in reads:
            for k, v in b.w.items():
                if need.get(k, 0) < v:
                    need[k] = v
        for b in writes:
            for dct in (b.w, b.r):
                for k, v in dct.items():
                    if k == eng.name:
                        continue
                    if need.get(k, 0) < v:
                        need[k] = v
        for k, v in need.items():
            self._wait(eng, k, v)

    def _record(self, key, val, reads, writes, add):
        for b in reads:
            if b.r.get(key, 0) < val:
                b.r[key] = val
        for b in writes:
            if add:
                if b.w.get(key, 0) < val:
                    b.w[key] = val
            else:
                b.w = {key: val}
                b.r = {}

    def I(self, en, emit, reads=(), writes=(), inc=True, add=False):
        eng = self.E[en]
        self._deps(eng, reads, writes)
        ins = emit(eng.e)
        self.ninstr += 1
        if inc:
            eng.count += 1
            ins.then_inc(eng.sem, 1)
            val = eng.count
            eng.pending = False
        else:
            val = eng.count + 1
            eng.pending = True
        self._record(eng.name, val, reads, writes, add)
        return ins

    def dma(self, out, in_, reads=(), writes=(), q="sp", add=False, **kw):
        eng = self.E[q]
        d = self.dsems[self.dnext]
        self.dnext = (self.dnext + 1) % self.NDMA
        self._wait(eng, d.name, d.count)
        self._deps(eng, reads, writes)
        ins = eng.e.dma_start(out=out, in_=in_, **kw)
        self.ninstr += 1
        d.count += 16
        ins.then_inc(d.sem, 16)
        self._record(d.name, d.count, reads, writes, add)
        return ins

    def barrier(self, engines=("pe", "act", "dve", "pool", "sp")):
        for n, eng in self.E.items():
            assert not eng.pending, n
        for n in engines:
            eng = self.E[n]
            for m, other in self.E.items():
                if m != n:
                    self._wait(eng, m, other.count)
            for d in self.dsems:
                self._wait(eng, d.name, d.count)

    def sb(self, es, name, shape, dt):
        self.nsb = getattr(self, "nsb", 0) + 1
        return es.enter_context(self.nc.sbuf_tensor("%s_%d" % (name, self.nsb), list(shape), dt))

    def init_psum(self, es):
        self.psum = []
        for i in range(8):
            t = es.enter_context(self.nc.psum_tensor("ps%d" % i, [128, 512], F32))
            self.psum.append((t, Dep()))
        self.psn = 0

    def ps(self):
        t = self.psum[self.psn]
        self.psn = (self.psn + 1) % 8
        return t


def eval_blocks():
    import os
    v = os.environ.get("MK_BLOCKS")
    return range(NB) if not v else [int(a) for a in v.split(",")]


def build_program(debug=(), upto="all", skip=()):
    nc = bass.Bass("TRN2", target_bir_lowering=False)

    def din(name, shape):
        return nc.dram_tensor(name, list(shape), F32, kind="ExternalInput").ap()

    def scratch(name, shape, dt):
        if name in debug:
            return nc.dram_tensor(name, list(shape), dt, kind="ExternalOutput").ap()
        return nc.dram_tensor(name, list(shape), dt).ap()

    xT = din("xT", [D, T])
    ccol = din("ccol", [128, KC, 2])
    mod_w = din("mod_w", [2, D, 6 * D])
    modb = din("modb", [128, 2, 48])
    ngc = din("ngc", [128, 2, 2, KC])
    ffn_w_in = din("ffn_w_in", [2, D, 2 * FH])
    ffn_w_out = din("ffn_w_out", [2, FH, D])
    even_w_in = din("even_w_in", [D, 3328])
    even_w_out = din("even_w_out", [D, D])
    odd_w_in = din("odd_w_in", [D, 6144])
    odd_w_out = din("odd_w_out", [2048, D])
    gqk = din("gqk", [128, 640])
    sink = din("sink", [1, 8])
    outg = din("outg", [128, 1])
    lbraw = din("lbraw", [128, 2, 4])
    rope0 = din("rope0", [T, 64])
    rc1 = din("rc1", [128, T])
    rs1 = din("rs1", [128, T])
    gtab = din("gtab", [16, 128, 512])
    cmask = din("cmask", [128, 6, 512])
    outT = nc.dram_tensor("outT", [D, L], F32, kind="ExternalOutput").ap()

    QT0 = scratch("QT0", [512, T], BF16)
    KT0 = scratch("KT0", [128, T], BF16)
    V0 = scratch("V0", [T, 128], BF16)
    QD = scratch("QD", [2, 512, T], BF16)
    KDF = scratch("KDF", [2, 512, T], BF16)
    KDT = scratch("KDT", [2, T, 512], BF16)
    A1 = scratch("A1", [2, 512, T // 64], F32)
    VI = scratch("VI", [T, 512], BF16)
    GS = scratch("GS", [512, T], F32)
    AT = scratch("AT", [512, T], BF16)
    OFW = scratch("OFW", [2, 512, T], F32)
    XA = scratch("XA", [D, T], F32)
    X1 = scratch("X1", [D, T], F32)
    RQD = scratch("RQD", [2, 1024, T], BF16)
    RKF = scratch("RKF", [2, 1024, T], BF16)
    RKT = scratch("RKT", [2, T, 1024], BF16)
    RV = scratch("RV", [T, 2048], BF16)
    RG = scratch("RG", [2048, T], F32)
    RO = scratch("RO", [2, 2048, T], F32)
    XB = scratch("XB", [D, T], F32)

    top = ExitStack()
    mk = MK(nc, top)
    mk.init_psum(top)
    I = mk.I

    MOD = mk.sb(top, "MOD", [128, 2, 6, KC, 2], F32); dMOD = Dep()
    GM = mk.sb(top, "GM", [128, 2, 2, KC, 2], F32); dGM = Dep()
    cm = mk.sb(top, "cm", [128, 6, 512], BF16); dcm = Dep()
    ones = mk.sb(top, "ones", [128, 3, 128], BF16); dones = Dep()
    epsc = mk.sb(top, "epsc", [128, 1], F32); deps_ = Dep()
    ngt = mk.sb(top, "ngt", [128, 2, 2, KC], F32); dng = Dep()
    mk.dma(cm[:], cmask[:, :, :], writes=[dcm], q="pool")
    mk.dma(ngt[:], ngc[:, :, :, :], writes=[dng])
    I("dve", lambda e: e.memset(ones[:, 0, :], 1.0 / 1024), writes=[dones])
    I("dve", lambda e: e.memset(ones[:, 1, :], 1.0 / 128), writes=[dones], add=True)
    I("dve", lambda e: e.memset(ones[:, 2, :], 1.0 / 512), writes=[dones], add=True)
    I("dve", lambda e: e.memset(epsc[:], EPS), writes=[deps_])
    ident = cm[:, 4, 0:128]
    MPREV, MNEXT = cm[:, 0, :], cm[:, 1, :]
    TRI = [cm[:, 2, 0:128], cm[:, 3, 0:128]]
    SCANM = None

    def MM(out, dps, lhsT, rhs, st, sp, rd):
        I("pe", lambda e: e.matmul(out, lhsT, rhs, start=st, stop=sp), reads=rd, writes=[dps], inc=sp, add=not st)

    def load_w(es, name, src, rows, cols, c0=0):
        n = rows // 128
        t = mk.sb(es, name, [128, n, cols], BF16)
        d = Dep()
        for k in range(n):
            mk.dma(t[:, k, :], src[k * 128:(k + 1) * 128, c0:c0 + cols], writes=[d], q="pool", add=(k > 0))
        return t, d

    with ExitStack() as es:
        cc = mk.sb(es, "cc", [128, KC, 2], F32); dcc = Dep()
        ccb = mk.sb(es, "ccb", [128, KC, 2], BF16); dccb = Dep()
        mbt = mk.sb(es, "mbt", [128, 2, 48], F32); dmb = Dep()
        mk.dma(cc[:], ccol[:, :, :], writes=[dcc])
        mk.dma(mbt[:], modb[:, :, :], writes=[dmb])
        I("act", lambda e: e.activation(out=ccb[:], in_=cc[:], func=AF.Silu), reads=[dcc], writes=[dccb])
        for l in range(2):
            for half in range(2):
                with ExitStack() as es2:
                    wm, dwm = load_w(es2, "wm", mod_w[l], D, 3072, c0=half * 3072)
                    ps, dps = mk.ps()
                    for fc in range(24):
                        for kc in range(KC):
                            MM(ps[:, fc * 2:fc * 2 + 2], dps, wm[:, kc, fc * 128:(fc + 1) * 128], ccb[:, kc, :],
                               kc == 0, kc == KC - 1, [dwm, dccb])
                    mv = MOD[:, l].rearrange("p j k w -> p (j k) w")[:, half * 24:(half + 1) * 24, :]
                    I("dve", lambda e: e.tensor_tensor(
                        mv, ps[:, 0:48].rearrange("p (f w) -> p f w", w=2),
                        mbt[:, l, half * 24:(half + 1) * 24].unsqueeze(2).to_broadcast([128, 24, 2]), ALU.add),
                      reads=[dps, dmb], writes=[dMOD], add=True)
                    mk.barrier()
        for l in range(2):
            for i, j in ((0, 1), (1, 4)):
                I("dve", lambda e: e.tensor_scalar(GM[:, l, i], MOD[:, l, j], 1.0, None, ALU.add),
                  reads=[dMOD], writes=[dGM], add=True)
                I("dve", lambda e: e.tensor_tensor(GM[:, l, i], GM[:, l, i],
                                                   ngt[:, l, i].unsqueeze(2).to_broadcast([128, KC, 2]), ALU.mult),
                  reads=[dGM, dng], writes=[dGM], add=True)
        mk.barrier()

    def norm_mod(xt, dxt, hb, dhb, sq, dsq, rst, drst, l, i, which, w):
        I("act", lambda e: e.activation(out=sq[:, :, 0:w], in_=xt[:, :, 0:w], func=AF.Square), reads=[dxt], writes=[dsq])
        ps, dps = mk.ps()
        for kc in range(KC):
            MM(ps[:, 0:w], dps, ones[:, 0, :], sq[:, kc, 0:w], kc == 0, kc == KC - 1, [dsq, dones])
        I("act", lambda e: e.activation(out=rst[:, 0:w], in_=ps[:, 0:w], func=AF.Ln, bias=epsc[:, 0:1], scale=1.0),
          reads=[dps, deps_], writes=[drst])
        I("act", lambda e: e.activation(out=rst[:, 0:w], in_=rst[:, 0:w], func=AF.Exp, scale=-0.5),
          reads=[drst], writes=[drst])
        sj = 0 if i == 0 else 3
        for kc in range(KC):
            en = "dve" if kc % 2 == 0 else "pool"
            I(en, lambda e: e.tensor_tensor(xn_[:, kc % 2, 0:w], xt[:, kc, 0:w], rst[:, 0:w], ALU.mult),
              reads=[dxt, drst], writes=[dxn_[kc % 2]])
            I("act", lambda e: e.activation(out=hb[:, kc, 0:w], in_=xn_[:, kc % 2, 0:w], func=AF.Identity,
                                            scale=GM[:, l, i, kc, which:which + 1],
                                            bias=MOD[:, l, sj, kc, which:which + 1]),
              reads=[dxn_[kc % 2], dGM, dMOD], writes=[dhb], add=(kc > 0))

    with ExitStack() as es:
        W0, dW0 = load_w(es, "W0", even_w_in, D, 3328)
        xt = mk.sb(es, "xt", [128, KC, 512], F32); dxt = Dep()
        xn_ = mk.sb(es, "xn", [128, 2, 512], F32); dxn_ = [Dep(), Dep()]
        sq = mk.sb(es, "sq", [128, KC, 512], BF16); dsq = Dep()
        rst = mk.sb(es, "rst", [128, 512], F32); drst = Dep()
        hb = mk.sb(es, "hb", [128, KC, 512], BF16); dhb = Dep()
        gq = mk.sb(es, "gq", [128, 640], F32); dgq = Dep()
        mk.dma(gq[:], gqk[:, :], writes=[dgq])
        lbr = mk.sb(es, "lbr", [128, 2, 4], F32); dlbr = Dep()
        LB = mk.sb(es, "LB", [128, 3, 4], F32); dLB = Dep()
        mk.dma(lbr[:], lbraw[:, :, :], writes=[dlbr])
        I("dve", lambda e: e.tensor_tensor(LB[:, 1, :], lbr[:, 0, :], lbr[:, 1, :], ALU.subtract), reads=[dlbr], writes=[dLB])
        I("act", lambda e: e.activation(out=LB[:, 0, :], in_=LB[:, 1, :], func=AF.Sigmoid), reads=[dLB], writes=[dLB], add=True)
        I("dve", lambda e: e.tensor_scalar(LB[:, 1, :], LB[:, 0, :], -1.0, 1.0, ALU.mult, ALU.add), reads=[dLB], writes=[dLB], add=True)
        I("dve", lambda e: e.tensor_scalar(LB[:, 2, :], LB[:, 0, :], -1.0, None, ALU.add), reads=[dLB], writes=[dLB], add=True)
        scm = mk.sb(es, "scm", [128, 512], F32); dscm = Dep()
        I("dve", lambda e: e.tensor_copy(scm[:], cm[:, 5, :]), reads=[dcm], writes=[dscm])
        rp = mk.sb(es, "rp", [128, 4, 64], F32); drp = Dep()
        sqa = mk.sb(es, "sqa", [128, 640], F32); dsqa = Dep()
        ssq = mk.sb(es, "ssq", [128, 10], F32); dssq = Dep()
        qn = mk.sb(es, "qn", [128, 640], F32); dqn = Dep()
        t1 = mk.sb(es, "t1", [128, 10, 32], F32); dt1 = Dep()
        t2 = mk.sb(es, "t2", [128, 10, 32], F32); dt2 = Dep()
        qr = mk.sb(es, "qr", [128, 640], BF16); dqr = Dep()
        qTs = mk.sb(es, "qTs", [128, 5, 512], BF16); dqTs = Dep()
        vst = mk.sb(es, "vst", [128, 4, 128], BF16); dvst = Dep()
        vis = mk.sb(es, "vis", [128, 4, 512], BF16); dvis = Dep()
        qs = mk.sb(es, "qs", [128, 4, 512], F32); dqs = [Dep() for _ in range(4)]
        rf = mk.sb(es, "rf", [128, 4, 2, 512], F32); drf = [Dep() for _ in range(4)]
        gsb = mk.sb(es, "gsb", [128, 2, 512], F32); dgsb = [Dep(), Dep()]
        lfp = mk.sb(es, "lfp", [128, 2, 520], F32); dlf = [Dep(), Dep()]
        kk = mk.sb(es, "kk", [128, 2, 512], F32); dkk = [Dep(), Dep()]
        cu = mk.sb(es, "cu", [128, 2, 512], F32); dcu = [Dep(), Dep()]
        ee = mk.sb(es, "ee", [128, 2, 2, 512], F32); dee = [Dep(), Dep()]
        qdb = mk.sb(es, "qdb", [128, 2, 512], BF16); dqdb = [Dep(), Dep()]
        kdb = mk.sb(es, "kdb", [128, 2, 512], BF16); dkdb = [Dep(), Dep()]
        kts = mk.sb(es, "kts", [128, 2, 4, 128], BF16); dkts = Dep()
        a1s = mk.sb(es, "a1s", [128, 2, 8], F32); da1s = Dep()
        a1t = mk.sb(es, "a1t", [128, 8], F32); da1t = Dep()
        I("dve", lambda e: e.memset(lfp[:], 0.0), writes=dlf)

        for (t0, w) in (TILES if "A" not in skip else []):
            which = 1 if t0 == 0 else 0
            nst = w // 128
            nch = w // 64
            mk.dma(xt[:, :, 0:w], xT[:, t0:t0 + w].rearrange("(k p) t -> p k t", p=128), writes=[dxt])
            norm_mod(xt, dxt, hb, dhb, sq, dsq, rst, drst, 0, 0, which, w)
            for st in range(nst):
                tk = slice(st * 128, (st + 1) * 128)
                pq, dpq = mk.ps()
                pk, dpk = mk.ps()
                for kc in range(KC):
                    MM(pq[:, 0:512], dpq, hb[:, kc, tk], W0[:, kc, 0:512], kc == 0, kc == KC - 1, [dhb, dW0])
                for kc in range(KC):
                    MM(pk[:, 0:256], dpk, hb[:, kc, tk], W0[:, kc, 512:768], kc == 0, kc == KC - 1, [dhb, dW0])
                mk.dma(rp[:, st, :], rope0[t0 + st * 128:t0 + (st + 1) * 128, :], writes=[drp], add=True)
                I("act", lambda e: e.activation(out=sqa[:, 0:512], in_=pq[:, 0:512], func=AF.Square), reads=[dpq], writes=[dsqa])
                I("act", lambda e: e.activation(out=sqa[:, 512:640], in_=pk[:, 0:128], func=AF.Square), reads=[dpk], writes=[dsqa], add=True)
                I("dve", lambda e: e.tensor_reduce(ssq[:], sqa[:].rearrange("p (h d) -> p h d", d=64), AX.X, ALU.add),
                  reads=[dsqa], writes=[dssq])
                I("act", lambda e: e.activation(out=ssq[:], in_=ssq[:], func=AF.Ln, bias=epsc[:, 0:1], scale=1.0 / 64),
                  reads=[dssq, deps_], writes=[dssq])
                I("act", lambda e: e.activation(out=ssq[:], in_=ssq[:], func=AF.Exp, scale=-0.5), reads=[dssq], writes=[dssq])
                I("dve", lambda e: e.tensor_tensor(qn[:, 0:512].rearrange("p (h d) -> p h d", d=64),
                                                   pq[:, 0:512].rearrange("p (h d) -> p h d", d=64),
                                                   ssq[:, 0:8].unsqueeze(2).to_broadcast([128, 8, 64]), ALU.mult),
                  reads=[dpq, dssq], writes=[dqn])
                I("dve", lambda e: e.tensor_tensor(qn[:, 512:640].rearrange("p (h d) -> p h d", d=64),
                                                   pk[:, 0:128].rearrange("p (h d) -> p h d", d=64),
                                                   ssq[:, 8:10].unsqueeze(2).to_broadcast([128, 2, 64]), ALU.mult),
                  reads=[dpk, dssq], writes=[dqn], add=True)
                I("dve", lambda e: e.tensor_copy(vst[:, st, :], pk[:, 128:256]), reads=[dpk], writes=[dvst], add=True)
                I("pool", lambda e: e.tensor_tensor(qn[:], qn[:], gq[:], ALU.mult), reads=[dqn, dgq], writes=[dqn])
                qv = qn[:].rearrange("p (h d) -> p h d", d=64)
                qrv = qr[:].rearrange("p (h d) -> p h d", d=64)
                cosb = rp[:, st, 0:32].unsqueeze(1).to_broadcast([128, 10, 32])
                sinb = rp[:, st, 32:64].unsqueeze(1).to_broadcast([128, 10, 32])
                x1, x2 = qv[:, :, 0:32], qv[:, :, 32:64]
                I("dve", lambda e: e.tensor_tensor(t1[:], x1, cosb, ALU.mult), reads=[dqn, drp], writes=[dt1])
                I("pool", lambda e: e.tensor_tensor(t2[:], x2, sinb, ALU.mult), reads=[dqn, drp], writes=[dt2])
                I("dve", lambda e: e.tensor_tensor(qrv[:, :, 0:32], t1[:], t2[:], ALU.subtract), reads=[dt1, dt2], writes=[dqr])
                I("pool", lambda e: e.tensor_tensor(t1[:], x2, cosb, ALU.mult), reads=[dqn, drp, dqr], writes=[dt1])
                I("dve", lambda e: e.tensor_tensor(t2[:], x1, sinb, ALU.mult), reads=[dqn, drp, dqr], writes=[dt2])
                I("pool", lambda e: e.tensor_tensor(qrv[:, :, 32:64], t1[:], t2[:], ALU.add), reads=[dt1, dt2], writes=[dqr], add=True)
                pt, dpt = mk.ps()
                ptb = pt[:].bitcast(BF16)
                for j in range(5):
                    I("pe", lambda e: e.transpose(ptb[:, j * 128:(j + 1) * 128], qr[:, j * 128:(j + 1) * 128], ident),
                      reads=[dqr, dcm], writes=[dpt], inc=(j == 4), add=(j > 0))
                I("act", lambda e: e.copy(qTs[:, :, tk], ptb[:, 0:640].rearrange("p (j t) -> p j t", t=128)),
                  reads=[dpt], writes=[dqTs], add=True)
                pv, dpv = mk.ps()
                for kc in range(KC):
                    MM(pv[:, 0:512], dpv, hb[:, kc, tk], W0[:, kc, 2304:2816], kc == 0, kc == KC - 1, [dhb, dW0])
                I("dve", lambda e: e.tensor_copy(vis[:, st, :], pv[:, 0:512]), reads=[dpv], writes=[dvis], add=True)
            mk.dma(QT0[:, t0:t0 + w].rearrange("(j p) t -> p j t", p=128), qTs[:, 0:4, 0:w], reads=[dqTs])
            mk.dma(KT0[:, t0:t0 + w], qTs[:, 4, 0:w], reads=[dqTs])
            mk.dma(V0[t0:t0 + w, :].rearrange("(s p) c -> p s c", p=128), vst[:, 0:nst, :], reads=[dvst])
            mk.dma(VI[t0:t0 + w, :].rearrange("(s p) c -> p s c", p=128), vis[:, 0:nst, :], reads=[dvis])
            for h in range(4):
                pss = []
                for base in (768, 2816, 1280, 1792):
                    p_, dp_ = mk.ps()
                    c0 = base + h * 128
                    for kc in range(KC):
                        MM(p_[:, 0:w], dp_, W0[:, kc, c0:c0 + 128], hb[:, kc, 0:w], kc == 0, kc == KC - 1, [dhb, dW0])
                    pss.append((p_, dp_))
                (pqq, dpqq), (pg, dpg), (pff, dpff), (pfb, dpfb) = pss
                gi = h % 2
                I("act", lambda e: e.activation(out=qs[:, h, 0:w], in_=pqq[:, 0:w], func=AF.Sigmoid), reads=[dpqq], writes=[dqs[h]])
                I("dve", lambda e: e.tensor_tensor(qs[:, h, 0:w], qs[:, h, 0:w], pqq[:, 0:w], ALU.mult), reads=[dqs[h], dpqq], writes=[dqs[h]])
                I("act", lambda e: e.activation(out=gsb[:, gi, 0:w], in_=pg[:, 0:w], func=AF.Sigmoid), reads=[dpg], writes=[dgsb[gi]])
                I("dve", lambda e: e.tensor_tensor(gsb[:, gi, 0:w], gsb[:, gi, 0:w], pg[:, 0:w], ALU.mult), reads=[dgsb[gi], dpg], writes=[dgsb[gi]])
                mk.dma(GS[h * 128:(h + 1) * 128, t0:t0 + w], gsb[:, gi, 0:w], reads=[dgsb[gi]])
                I("act", lambda e: e.activation(out=rf[:, h, 0, 0:w], in_=pff[:, 0:w], func=AF.Sigmoid), reads=[dpff], writes=[drf[h]])
                I("act", lambda e: e.activation(out=rf[:, h, 1, 0:w], in_=pfb[:, 0:w], func=AF.Sigmoid), reads=[dpfb], writes=[drf[h]], add=True)
            for h in range(4):
                for dr in range(2):
                    r_ = rf[:, h, dr, 0:w]
                    lf = lfp[:, dr, 1:1 + w]
                    I("act", lambda e: e.activation(out=lf, in_=r_, func=AF.Ln, scale=LB[:, 1, h:h + 1], bias=LB[:, 0, h:h + 1]),
                      reads=[drf[h], dLB], writes=[dlf[dr]])
                    I("pool", lambda e: e.tensor_scalar(kk[:, dr, 0:w], r_, LB[:, 2, h:h + 1], LB[:, 1, h:h + 1], ALU.mult, ALU.add),
                      reads=[drf[h], dLB], writes=[dkk[dr]])
                    if dr == 0:
                        I("dve", lambda e: e.tensor_tensor_scan(cu[:, dr, 0:w], scm[:, 0:w], lf, 0.0, ALU.mult, ALU.add),
                          reads=[dscm, dlf[dr]], writes=[dcu[dr]])
                        a1src = ee[:, dr, 0, 0:w]
                    else:
                        I("dve", lambda e: e.tensor_tensor_scan(cu[:, dr, 0:w], lfp[:, dr, 0:w], scm[:, 0:w], 0.0, ALU.add, ALU.mult),
                          reads=[dscm, dlf[dr]], writes=[dcu[dr]])
                    sgn = (1.0, -1.0) if dr == 0 else (-1.0, 1.0)
                    I("act", lambda e: e.activation(out=ee[:, dr, 0, 0:w], in_=cu[:, dr, 0:w], func=AF.Exp, scale=sgn[0]),
                      reads=[dcu[dr]], writes=[dee[dr]])
                    I("act", lambda e: e.activation(out=ee[:, dr, 1, 0:w], in_=cu[:, dr, 0:w], func=AF.Exp, scale=sgn[1]),
                      reads=[dcu[dr]], writes=[dee[dr]], add=True)
                    I("dve", lambda e: e.tensor_tensor(qdb[:, dr, 0:w], qs[:, h, 0:w], ee[:, dr, 0, 0:w], ALU.mult),
                      reads=[dqs[h], dee[dr]], writes=[dqdb[dr]])
                    I("pool", lambda e: e.tensor_tensor(kdb[:, dr, 0:w], kk[:, dr, 0:w], ee[:, dr, 1, 0:w], ALU.mult),
                      reads=[dkk[dr], dee[dr]], writes=[dkdb[dr]])
                    if dr == 0:
                        I("dve", lambda e: e.tensor_copy(a1s[:, dr, 0:nch], ee[:, dr, 0, 0:w].rearrange("p (c t) -> p c t", t=64)[:, :, 63]),
                          reads=[dee[dr]], writes=[da1s])
                    else:
                        I("dve", lambda e: e.tensor_tensor(a1t[:, 0:nch], cu[:, dr, 0:w].rearrange("p (c t) -> p c t", t=64)[:, :, 63],
                                                           lfp[:, dr, 1:1 + w].rearrange("p (c t) -> p c t", t=64)[:, :, 63], ALU.add),
                          reads=[dcu[dr], dlf[dr]], writes=[da1t])
                        I("act", lambda e: e.activation(out=a1s[:, dr, 0:nch], in_=a1t[:, 0:nch], func=AF.Exp),
                          reads=[da1t], writes=[da1s], add=True)
                    mk.dma(QD[dr, h * 128:(h + 1) * 128, t0:t0 + w], qdb[:, dr, 0:w], reads=[dqdb[dr]])
                    mk.dma(KDF[dr, h * 128:(h + 1) * 128, t0:t0 + w], kdb[:, dr, 0:w], reads=[dkdb[dr]])
                mk.dma(A1[:, h * 128:(h + 1) * 128, t0 // 64:t0 // 64 + nch].rearrange("r p c -> p r c"), a1s[:, :, 0:nch], reads=[da1s])
                pt, dpt = mk.ps()
                ptb = pt[:].bitcast(BF16)
                n = 0
                for dr in range(2):
                    for st in range(nst):
                        I("pe", lambda e: e.transpose(ptb[:, (dr * 4 + st) * 128:(dr * 4 + st + 1) * 128],
                                                      kdb[:, dr, st * 128:(st + 1) * 128], ident),
                          reads=[dkdb[dr], dcm], writes=[dpt], inc=(n == 2 * nst - 1), add=(n > 0))
                        n += 1
                I("act", lambda e: e.copy(kts[:, :, 0:nst, :], ptb[:, 0:1024].rearrange("p (r s d) -> p r s d", r=2, s=4)[:, :, 0:nst, :]),
                  reads=[dpt], writes=[dkts])
                for dr in range(2):
                    mk.dma(KDT[dr, t0:t0 + w, h * 128:(h + 1) * 128].rearrange("(s p) d -> p s d", p=128), kts[:, dr, 0:nst, :], reads=[dkts])
        mk.barrier()
    if upto == "A":
        return finish(nc, mk, top)

    with ExitStack() as es:
        KTr = mk.sb(es, "KTr", [128, T], BF16); dKT = Dep()
        mk.dma(KTr[:], KT0[:, :], writes=[dKT])
        VA = mk.sb(es, "VA", [128, NB, 2, 128], BF16); dVA = Dep()
        I("pool", lambda e: e.memset(VA[:], 1.0), writes=[dVA])
        for b0 in range(0, NB, 22):
            for kv in range(2):
                mk.dma(VA[:, b0:b0 + 22, kv, 0:64],
                       V0[b0 * 128:(b0 + 22) * 128, kv * 64:(kv + 1) * 64].rearrange("(b p) d -> p b d", p=128), writes=[dVA], add=True)
        skr = mk.sb(es, "skr", [1, 8], F32); dsk = Dep()
        srow = mk.sb(es, "srow", [1, 8, 128], BF16); dsrow = Dep()
        sel = mk.sb(es, "sel", [1, 128], BF16); dsel = Dep()
        mk.dma(skr[:], sink[:, :], writes=[dsk])
        I("act", lambda e: e.activation(out=skr[:], in_=skr[:], func=AF.Exp), reads=[dsk], writes=[dsk])
        I("dve", lambda e: e.tensor_copy(srow[:], skr[:].unsqueeze(2).to_broadcast([1, 8, 128])), reads=[dsk], writes=[dsrow])
        I("dve", lambda e: e.memset(sel[:, 0:64], 0.0), writes=[dsel])
        I("dve", lambda e: e.memset(sel[:, 64:128], 1.0), writes=[dsel], add=True)
        qts = [mk.sb(es, "qt", [128, 4, 128], BF16) for _ in range(2)]; dqts = [Dep(), Dep()]
        pts = [mk.sb(es, "pt", [128, 512], BF16) for _ in range(6)]; dpts = [Dep() for _ in range(6)]
        rcs = mk.sb(es, "rc", [64, 2, 512], F32); drc = [Dep(), Dep()]
        aos = [mk.sb(es, "ao", [64, 2, 512], BF16) for _ in range(2)]; daos = [[Dep(), Dep()], [Dep(), Dep()]]
        pn = 0
        for gb in (eval_blocks() if "B" not in skip else []):
            qt, dqt = qts[gb % 2], dqts[gb % 2]
            tb = slice(gb * 128, (gb + 1) * 128)
            for kv in range(2):
                mk.dma(qt[kv * 64:(kv + 1) * 64, :, :],
                       QT0[kv * 256:(kv + 1) * 256, tb].rearrange("(g d) t -> d g t", d=64), writes=[dqt], add=(kv > 0))
            for kv in range(2):
                pr = slice(kv * 64, (kv + 1) * 64)
                if gb < 2:
                    keys = [(0, None), (1, None)]
                else:
                    keys = []
                    if gb > 2:
                        keys.append((gb - 1, MPREV))
                    keys.append((gb, None))
                    if gb < NB - 1:
                        keys.append((gb + 1, MNEXT))
                    keys += [(0, None), (1, None)]
                used = []
                for (kb, msk) in keys:
                    ps_, dps_ = mk.ps()
                    MM(ps_[:, 0:512], dps_, KTr[pr, kb * 128:(kb + 1) * 128], qt[pr, :, :].rearrange("p g t -> p (g t)"),
                       True, True, [dKT, dqt])
                    pt_, dpt_ = pts[pn % 6], dpts[pn % 6]
                    pn += 1
                    I("act", lambda e: e.activation(out=pt_[:], in_=ps_[:, 0:512], func=AF.Exp, scale=0.125), reads=[dps_], writes=[dpt_])
                    if msk is not None:
                        I("pool", lambda e: e.tensor_tensor(pt_[:], pt_[:], msk, ALU.mult), reads=[dpt_, dcm], writes=[dpt_])
                    used.append((kb, pt_, dpt_))
                po, dpo = mk.ps()
                for i_, (kb, pt_, dpt_) in enumerate(used):
                    MM(po[:, 0:512], dpo, VA[:, kb, kv, :], pt_[:], i_ == 0, False, [dVA, dpt_])
                MM(po[:, 0:512], dpo, sel[0:1, :], srow[0:1, kv * 4:(kv + 1) * 4, :].rearrange("p g t -> p (g t)"),
                   False, True, [dsel, dsrow])
                ao, dao = aos[gb % 2], daos[gb % 2][kv]
                I("dve", lambda e: e.reciprocal(rcs[0:64, kv, :], po[64:128, 0:512]), reads=[dpo], writes=[drc[kv]])
                I("dve", lambda e: e.tensor_tensor(ao[0:64, kv, :], po[0:64, 0:512], rcs[0:64, kv, :], ALU.mult),
                  reads=[dpo, drc[kv]], writes=[dao])
                mk.dma(AT[kv * 256:(kv + 1) * 256, tb].rearrange("(g d) t -> d g t", d=64),
                       ao[0:64, kv, :].rearrange("p (g t) -> p g t", t=128), reads=[dao])
        mk.barrier()
    if upto == "B":
        return finish(nc, mk, top)

    with ExitStack() as es:
        SA = mk.sb(es, "SA", [128, 8, 128], F32); SR = mk.sb(es, "SR", [128, 8, 128], F32)
        Sb = mk.sb(es, "Sb", [128, 8, 128], BF16)
        dSA = [Dep() for _ in range(8)]; dSR = [Dep() for _ in range(8)]; dSb = [Dep() for _ in range(8)]
        I("dve", lambda e: e.memset(SA[:], 0.0), writes=dSA)
        I("dve", lambda e: e.memset(Sb[:], 0.0), writes=dSb)
        am = mk.sb(es, "am", [64, 8, 64], BF16); dam = [Dep() for _ in range(8)]
        bufs = {}
        for par in range(2):
            for sidx in range(8):
                bufs[(par, sidx)] = dict(
                    qd=mk.sb(es, "gqd", [128, 512], BF16), kdf=mk.sb(es, "gkf", [128, 512], BF16),
                    kdt=mk.sb(es, "gkt", [64, 8, 128], BF16), v=mk.sb(es, "gv", [64, 8, 128], BF16),
                    a1=mk.sb(es, "ga1", [128, 8], F32), ob=mk.sb(es, "gob", [128, 512], F32),
                    dqd=Dep(), dkdf=Dep(), dkdt=Dep(), dv=Dep(), da1=Dep(), dob=Dep())
        order = [list(range(17)), [0] + list(range(16, 0, -1))]
        for step in (range(17) if "C" not in skip else []):
            par = step % 2
            cur = {}
            for dr in range(2):
                t0, w = TILES[order[dr][step]]
                nch = w // 64
                for h in range(4):
                    sidx = dr * 4 + h
                    B = bufs[(par, sidx)]
                    hs = slice(h * 128, (h + 1) * 128)
                    mk.dma(B["qd"][:, 0:w], QD[dr, hs, t0:t0 + w], writes=[B["dqd"]])
                    mk.dma(B["kdf"][:, 0:w], KDF[dr, hs, t0:t0 + w], writes=[B["dkdf"]])
                    mk.dma(B["kdt"][:, 0:nch, :], KDT[dr, t0:t0 + w, hs].rearrange("(c p) d -> p c d", p=64), writes=[B["dkdt"]])
                    mk.dma(B["v"][:, 0:nch, :], VI[t0:t0 + w, hs].rearrange("(c p) d -> p c d", p=64), writes=[B["dv"]])
                    mk.dma(B["a1"][:, 0:nch], A1[dr, hs, t0 // 64:t0 // 64 + nch], writes=[B["da1"]])
                    cur[sidx] = (B, t0, w, nch)
            for c in range(8):
                for dr in range(2):
                    for h in range(4):
                        sidx = dr * 4 + h
                        B, t0, w, nch = cur[sidx]
                        if c >= nch:
                            continue
                        cc = c if dr == 0 else nch - 1 - c
                        tk = slice(cc * 64, cc * 64 + 64)
                        a1c = B["a1"][:, cc:cc + 1]
                        sa, sr, sb_ = SA[:, sidx, :], SR[:, sidx, :], Sb[:, sidx, :]
                        if dr == 1:
                            I("act", lambda e: e.activation(out=sb_, in_=sa, func=AF.Identity, scale=a1c),
                              reads=[dSA[sidx], B["da1"]], writes=[dSb[sidx]])
                            I("pool", lambda e: e.tensor_scalar(sr, sa, a1c, None, ALU.mult),
                              reads=[dSA[sidx], B["da1"]], writes=[dSR[sidx]])
                        pa, dpa = mk.ps()
                        MM(pa[0:64, 0:64], dpa, B["kdf"][:, tk], B["qd"][:, tk], True, True, [B["dkdf"], B["dqd"]])
                        I("dve", lambda e: e.tensor_tensor(am[:, sidx, :], pa[0:64, 0:64], TRI[dr][0:64, 0:64], ALU.mult),
                          reads=[dpa, dcm], writes=[dam[sidx]])
                        MM(pa[:, 64:128], dpa, B["v"][:, cc, :], am[:, sidx, :], True, False, [B["dv"], dam[sidx]])
                        MM(pa[:, 64:128], dpa, sb_, B["qd"][:, tk], False, True, [dSb[sidx], B["dqd"]])
                        pu_, dpu_ = mk.ps()
                        MM(pu_[:, 0:128], dpu_, B["kdt"][:, cc, :], B["v"][:, cc, :], True, True, [B["dkdt"], B["dv"]])
                        I("act", lambda e: e.copy(B["ob"][:, tk], pa[:, 64:128]), reads=[dpa], writes=[B["dob"]], add=True)
                        if dr == 0:
                            I("dve", lambda e: e.tensor_tensor(sr, sa, pu_[:, 0:128], ALU.add),
                              reads=[dSA[sidx], dpu_], writes=[dSR[sidx]])
                            I("pool", lambda e: e.tensor_scalar(sa, sr, a1c, None, ALU.mult),
                              reads=[dSR[sidx], B["da1"]], writes=[dSA[sidx]])
                            I("act", lambda e: e.activation(out=sb_, in_=sr, func=AF.Identity, scale=a1c),
                              reads=[dSR[sidx], B["da1"]], writes=[dSb[sidx]])
                        else:
                            I("dve", lambda e: e.tensor_tensor(sa, sr, pu_[:, 0:128], ALU.add),
                              reads=[dSR[sidx], dpu_], writes=[dSA[sidx]])
            for sidx in range(8):
                B, t0, w, nch = cur[sidx]
                dr, h = sidx // 4, sidx % 4
                mk.dma(OFW[dr, h * 128:(h + 1) * 128, t0:t0 + w], B["ob"][:, 0:w], reads=[B["dob"]])
        mk.barrier()
    if upto == "C":
        return finish(nc, mk, top)

    with ExitStack() as es:
        Wo, dWo = load_w(es, "Wo", even_w_out, D, D)
        ogt = mk.sb(es, "ogt", [128, 1], F32); dog = Dep()
        mk.dma(ogt[:], outg[:, :], writes=[dog])
        xt = mk.sb(es, "xt", [128, KC, 512], F32); dxt = Dep()
        at = mk.sb(es, "at", [128, 4, 512], BF16); dat = Dep()
        of = mk.sb(es, "of", [128, 2, 4, 512], F32); dof = Dep()
        gs = mk.sb(es, "gs", [128, 4, 512], F32); dgs = Dep()
        bs = mk.sb(es, "bs", [128, 4, 512], F32); dbs = Dep()
        sqb = mk.sb(es, "sqb", [128, 4, 512], BF16); dsqb = Dep()
        rs4 = mk.sb(es, "rs4", [128, 4, 512], F32); drs4 = Dep()
        bl = mk.sb(es, "bl", [128, 4, 512], BF16); dbl = Dep()
        for (t0, w) in (TILES if "D1" not in skip else []):
            which = 1 if t0 == 0 else 0
            mk.dma(xt[:, :, 0:w], xT[:, t0:t0 + w].rearrange("(k p) t -> p k t", p=128), writes=[dxt])
            mk.dma(at[:, :, 0:w], AT[:, t0:t0 + w].rearrange("(k p) t -> p k t", p=128), writes=[dat])
            for dr in range(2):
                mk.dma(of[:, dr, :, 0:w], OFW[dr, :, t0:t0 + w].rearrange("(k p) t -> p k t", p=128), writes=[dof], add=(dr > 0))
            mk.dma(gs[:, :, 0:w], GS[:, t0:t0 + w].rearrange("(k p) t -> p k t", p=128), writes=[dgs])
            I("dve", lambda e: e.tensor_tensor(bs[:, :, 0:w], of[:, 0, :, 0:w], of[:, 1, :, 0:w], ALU.add), reads=[dof], writes=[dbs])
            I("act", lambda e: e.activation(out=sqb[:, :, 0:w], in_=bs[:, :, 0:w], func=AF.Square), reads=[dbs], writes=[dsqb])
            for h in range(4):
                ps_, dps_ = mk.ps()
                MM(ps_[:, 0:w], dps_, ones[:, 1, :], sqb[:, h, 0:w], True, True, [dsqb, dones])
                I("act", lambda e: e.activation(out=rs4[:, h, 0:w], in_=ps_[:, 0:w], func=AF.Ln, bias=epsc[:, 0:1], scale=1.0),
                  reads=[dps_, deps_], writes=[drs4], add=(h > 0))
            I("act", lambda e: e.activation(out=rs4[:, :, 0:w], in_=rs4[:, :, 0:w], func=AF.Exp, scale=-0.5), reads=[drs4], writes=[drs4])
            I("pool", lambda e: e.tensor_tensor(bs[:, :, 0:w], bs[:, :, 0:w], rs4[:, :, 0:w], ALU.mult), reads=[dbs, drs4], writes=[dbs])
            I("dve", lambda e: e.scalar_tensor_tensor(bl[:, :, 0:w], bs[:, :, 0:w], ogt[:, 0:1], gs[:, :, 0:w], ALU.mult, ALU.mult),
              reads=[dbs, dog, dgs], writes=[dbl])
            for fc in range(KC):
                ps_, dps_ = mk.ps()
                for ic in range(8):
                    rhs = at[:, ic, 0:w] if ic < 4 else bl[:, ic - 4, 0:w]
                    MM(ps_[:, 0:w], dps_, Wo[:, ic, fc * 128:(fc + 1) * 128], rhs, ic == 0, ic == 7, [dWo, dat, dbl])
                I("dve", lambda e: e.scalar_tensor_tensor(xt[:, fc, 0:w], ps_[:, 0:w], MOD[:, 0, 2, fc, which:which + 1],
                                                          xt[:, fc, 0:w], ALU.mult, ALU.add),
                  reads=[dps_, dMOD, dxt], writes=[dxt])
            mk.dma(XA[:, t0:t0 + w].rearrange("(k p) t -> p k t", p=128), xt[:, :, 0:w], reads=[dxt])
        mk.barrier()
    if upto == "D1":
        return finish(nc, mk, top)

    def ffn_phase(l, src, dst, final):
        nonlocal xn_, dxn_
        with ExitStack() as es:
            Wi, dWi = load_w(es, "Wi", ffn_w_in[l], D, 2 * FH)
            Wf, dWf = load_w(es, "Wf", ffn_w_out[l], FH, D)
            xt = mk.sb(es, "xt", [128, KC, 512], F32); dxt = Dep()
            xn_ = mk.sb(es, "xn", [128, 2, 512], F32); dxn_ = [Dep(), Dep()]
            sq = mk.sb(es, "sq", [128, KC, 512], BF16); dsq = Dep()
            rst = mk.sb(es, "rst", [128, 512], F32); drst = Dep()
            hb = mk.sb(es, "hb", [128, KC, 512], BF16); dhb = Dep()
            act = mk.sb(es, "act", [128, 22, 512], BF16); dact = Dep()
            sg = mk.sb(es, "sg", [128, 2, 512], F32); dsg = [Dep(), Dep()]
            for (t0, w) in (TILES if "D2" not in skip else []):
                if final and t0 == 0:
                    continue
                which = 1 if t0 == 0 else 0
                mk.dma(xt[:, :, 0:w], src[:, t0:t0 + w].rearrange("(k p) t -> p k t", p=128), writes=[dxt])
                norm_mod(xt, dxt, hb, dhb, sq, dsq, rst, drst, l, 1, which, w)
                for hc in range(22):
                    pg, dpg = mk.ps()
                    pu, dpu = mk.ps()
                    for kc in range(KC):
                        MM(pg[:, 0:w], dpg, Wi[:, kc, hc * 128:(hc + 1) * 128], hb[:, kc, 0:w], kc == 0, kc == KC - 1, [dWi, dhb])
                    for kc in range(KC):
                        MM(pu[:, 0:w], dpu, Wi[:, kc, FH + hc * 128:FH + (hc + 1) * 128], hb[:, kc, 0:w], kc == 0, kc == KC - 1, [dWi, dhb])
                    I("act", lambda e: e.activation(out=sg[:, hc % 2, 0:w], in_=pg[:, 0:w], func=AF.Silu), reads=[dpg], writes=[dsg[hc % 2]])
                    I("dve", lambda e: e.tensor_tensor(act[:, hc, 0:w], sg[:, hc % 2, 0:w], pu[:, 0:w], ALU.mult),
                      reads=[dsg[hc % 2], dpu], writes=[dact], add=(hc > 0))
                for fc in range(KC):
                    ps_, dps_ = mk.ps()
                    for hc in range(22):
                        MM(ps_[:, 0:w], dps_, Wf[:, hc, fc * 128:(fc + 1) * 128], act[:, hc, 0:w], hc == 0, hc == 21, [dWf, dact])
                    I("dve", lambda e: e.scalar_tensor_tensor(xt[:, fc, 0:w], ps_[:, 0:w], MOD[:, l, 5, fc, which:which + 1],
                                                              xt[:, fc, 0:w], ALU.mult, ALU.add),
                      reads=[dps_, dMOD, dxt], writes=[dxt])
                if final:
                    mk.dma(dst[:, t0 - LC:t0 - LC + w].rearrange("(k p) t -> p k t", p=128), xt[:, :, 0:w], reads=[dxt])
                else:
                    mk.dma(dst[:, t0:t0 + w].rearrange("(k p) t -> p k t", p=128), xt[:, :, 0:w], reads=[dxt])
            mk.barrier()

    ffn_phase(0, XA, X1, False)
    if upto == "D2":
        return finish(nc, mk, top)

    import math
    LG = [[math.log(1.0 - 2.0 ** (-5.0 - h)) for h in range(4)]]
    LG.append(LG[0][::-1])

    with ExitStack() as es:
        W1, dW1 = load_w(es, "W1", odd_w_in, D, 6144)
        xt = mk.sb(es, "xt", [128, KC, 512], F32); dxt = Dep()
        xn_ = mk.sb(es, "xn", [128, 2, 512], F32); dxn_ = [Dep(), Dep()]
        sq = mk.sb(es, "sq", [128, KC, 512], BF16); dsq = Dep()
        rst = mk.sb(es, "rst", [128, 512], F32); drst = Dep()
        hb = mk.sb(es, "hb", [128, KC, 512], BF16); dhb = Dep()
        cs = mk.sb(es, "cs", [128, 2, 512], F32); dcs = Dep()
        gt4 = mk.sb(es, "gt4", [128, 4, 512], F32); dgt4 = Dep()
        x12 = mk.sb(es, "x12", [128, 2, 512], F32); dx12 = Dep()
        o12 = mk.sb(es, "o12", [128, 2, 512], F32); do12 = [Dep(), Dep()]
        tA = mk.sb(es, "tA", [128, 512], F32); dtA = Dep()
        tB = mk.sb(es, "tB", [128, 512], F32); dtB = Dep()
        qdo = mk.sb(es, "qdo", [128, 2, 2, 2, 512], BF16); dqdo = [Dep(), Dep()]
        kt2 = mk.sb(es, "kt2", [128, 2, 4, 256], BF16); dkt2 = [Dep(), Dep()]
        vs = mk.sb(es, "vs", [128, 2, 2048], BF16); dvs = [Dep(), Dep()]
        gsb = mk.sb(es, "gsb", [128, 2, 512], F32); dgsb = [Dep(), Dep()]
        for (t0, w) in (TILES if "E" not in skip else []):
            which = 1 if t0 == 0 else 0
            nst = w // 128
            mk.dma(xt[:, :, 0:w], X1[:, t0:t0 + w].rearrange("(k p) t -> p k t", p=128), writes=[dxt])
            norm_mod(xt, dxt, hb, dhb, sq, dsq, rst, drst, 1, 0, which, w)
            mk.dma(cs[:, 0, 0:w], rc1[:, t0:t0 + w], writes=[dcs])
            mk.dma(cs[:, 1, 0:w], rs1[:, t0:t0 + w], writes=[dcs], add=True)
            for h in range(4):
                mk.dma(gt4[:], gtab[h * 4:(h + 1) * 4].rearrange("i p c -> p i c"), writes=[dgt4])
                for qk in range(2):
                    c0 = qk * 1024 + h * 256
                    pp = []
                    for half in range(2):
                        p_, dp_ = mk.ps()
                        for kc in range(KC):
                            MM(p_[:, 0:w], dp_, W1[:, kc, c0 + half * 128:c0 + (half + 1) * 128], hb[:, kc, 0:w],
                               kc == 0, kc == KC - 1, [dW1, dhb])
                        pp.append((p_, dp_))
                    I("act", lambda e: e.copy(x12[:, 0, 0:w], pp[0][0][:, 0:w]), reads=[pp[0][1]], writes=[dx12])
                    I("act", lambda e: e.copy(x12[:, 1, 0:w], pp[1][0][:, 0:w]), reads=[pp[1][1]], writes=[dx12], add=True)
                    x1, x2 = x12[:, 0, 0:w], x12[:, 1, 0:w]
                    cosv, sinv = cs[:, 0, 0:w], cs[:, 1, 0:w]
                    I("dve", lambda e: e.tensor_tensor(tA[:, 0:w], x1, cosv, ALU.mult), reads=[dx12, dcs], writes=[dtA])
                    I("pool", lambda e: e.tensor_tensor(tB[:, 0:w], x2, sinv, ALU.mult), reads=[dx12, dcs], writes=[dtB])
                    I("dve", lambda e: e.tensor_tensor(o12[:, 0, 0:w], tA[:, 0:w], tB[:, 0:w], ALU.subtract), reads=[dtA, dtB], writes=[do12[0]])
                    I("pool", lambda e: e.tensor_tensor(tA[:, 0:w], x2, cosv, ALU.mult), reads=[dx12, dcs, do12[0]], writes=[dtA])
                    I("dve", lambda e: e.tensor_tensor(tB[:, 0:w], x1, sinv, ALU.mult), reads=[dx12, dcs, do12[0]], writes=[dtB])
                    I("pool", lambda e: e.tensor_tensor(o12[:, 1, 0:w], tA[:, 0:w], tB[:, 0:w], ALU.add), reads=[dtA, dtB], writes=[do12[1]])
                    n = 0
                    for dr in range(2):
                        g0 = 0 if (dr == 0 or w == 512) else 256
                        for half in range(2):
                            en = "dve" if n % 2 == 0 else "pool"
                            I(en, lambda e: e.tensor_tensor(qdo[:, qk, dr, half, 0:w], o12[:, half, 0:w], gt4[:, dr * 2 + qk, g0:g0 + w], ALU.mult),
                              reads=[do12[half], dgt4], writes=[dqdo[qk]], add=(n > 0))
                            n += 1
                    dst = RQD if qk == 0 else RKF
                    for dr in range(2):
                        mk.dma(dst[dr, h * 256:(h + 1) * 256, t0:t0 + w].rearrange("(f p) t -> p f t", p=128),
                               qdo[:, qk, dr, :, 0:w], reads=[dqdo[qk]])
                    if qk == 1:
                        for dr in range(2):
                            pt, dpt = mk.ps()
                            ptb = pt[:].bitcast(BF16)
                            n = 0
                            for st in range(nst):
                                for half in range(2):
                                    I("pe", lambda e: e.transpose(ptb[:, st * 256 + half * 128:st * 256 + (half + 1) * 128],
                                                                  qdo[:, 1, dr, half, st * 128:(st + 1) * 128], ident),
                                      reads=[dqdo[1], dcm], writes=[dpt], inc=(n == 2 * nst - 1), add=(n > 0))
                                    n += 1
                            I("act", lambda e: e.copy(kt2[:, dr, 0:nst, :], ptb[:, 0:nst * 256].rearrange("p (s d) -> p s d", d=256)),
                              reads=[dpt], writes=[dkt2[dr]])
                            mk.dma(RKT[dr, t0:t0 + w, h * 256:(h + 1) * 256].rearrange("(s p) d -> p s d", p=128),
                                   kt2[:, dr, 0:nst, :], reads=[dkt2[dr]])
            for st in range(nst):
                tk = slice(st * 128, (st + 1) * 128)
                for j in range(4):
                    p_, dp_ = mk.ps()
                    for kc in range(KC):
                        MM(p_[:, 0:512], dp_, hb[:, kc, tk], W1[:, kc, 2048 + j * 512:2048 + (j + 1) * 512], kc == 0, kc == KC - 1, [dhb, dW1])
                    if j % 2 == 0:
                        I("act", lambda e: e.copy(vs[:, st % 2, j * 512:(j + 1) * 512], p_[:, 0:512]), reads=[dp_], writes=[dvs[st % 2]], add=(j > 0))
                    else:
                        I("dve", lambda e: e.tensor_copy(vs[:, st % 2, j * 512:(j + 1) * 512], p_[:, 0:512]), reads=[dp_], writes=[dvs[st % 2]], add=True)
                mk.dma(RV[t0 + st * 128:t0 + (st + 1) * 128, :], vs[:, st % 2, :], reads=[dvs[st % 2]])
            if t0 != 0:
                for gc in range(16):
                    p_, dp_ = mk.ps()
                    for kc in range(KC):
                        MM(p_[:, 0:w], dp_, W1[:, kc, 4096 + gc * 128:4096 + (gc + 1) * 128], hb[:, kc, 0:w], kc == 0, kc == KC - 1, [dW1, dhb])
                    gi = gc % 2
                    I("act", lambda e: e.activation(out=gsb[:, gi, 0:w], in_=p_[:, 0:w], func=AF.Sigmoid), reads=[dp_], writes=[dgsb[gi]])
                    I("dve", lambda e: e.tensor_tensor(gsb[:, gi, 0:w], gsb[:, gi, 0:w], p_[:, 0:w], ALU.mult), reads=[dgsb[gi], dp_], writes=[dgsb[gi]])
                    mk.dma(RG[gc * 128:(gc + 1) * 128, t0:t0 + w], gsb[:, gi, 0:w], reads=[dgsb[gi]])
        mk.barrier()
    if upto == "E":
        return finish(nc, mk, top)

    with ExitStack() as es:
        SA = mk.sb(es, "RSA", [128, 8, 2, 512], F32); SR = mk.sb(es, "RSR", [128, 8, 2, 512], F32)
        Sb = mk.sb(es, "RSb", [128, 8, 2, 512], BF16)
        dSA = [Dep() for _ in range(8)]; dSR = [Dep() for _ in range(8)]; dSb = [Dep() for _ in range(8)]
        I("dve", lambda e: e.memset(SA[:], 0.0), writes=dSA)
        I("pool", lambda e: e.memset(Sb[:], 0.0), writes=dSb)
        fb = []
        for par in range(2):
            fb.append(dict(qd=mk.sb(es, "fqd", [128, 2, 512], BF16), kf=mk.sb(es, "fkf", [128, 2, 512], BF16),
                           kt=mk.sb(es, "fkt", [128, 4, 256], BF16), v=mk.sb(es, "fv", [128, 4, 512], BF16),
                           am=mk.sb(es, "fam", [128, 4, 512], BF16), ob=mk.sb(es, "fob", [128, 4, 512], F32),
                           dqd=Dep(), dkf=Dep(), dkt=Dep(), dv=Dep(), dam=Dep(), dob=Dep()))
        order = [list(range(17)), [0] + list(range(16, 0, -1))]
        cnt = 0
        for step in (range(17) if "F" not in skip else []):
            for dr in range(2):
                t0, w = TILES[order[dr][step]]
                nst = w // 128
                for h in range(4):
                    sidx = dr * 4 + h
                    B = fb[cnt % 2]
                    cnt += 1
                    mk.dma(B["kt"][:, 0:nst, :], RKT[dr, t0:t0 + w, h * 256:(h + 1) * 256].rearrange("(s p) d -> p s d", p=128), writes=[B["dkt"]])
                    mk.dma(B["v"][:, 0:nst, :], RV[t0:t0 + w, h * 512:(h + 1) * 512].rearrange("(s p) e -> p s e", p=128), writes=[B["dv"]])
                    if t0 != 0:
                        mk.dma(B["qd"][:, :, 0:w], RQD[dr, h * 256:(h + 1) * 256, t0:t0 + w].rearrange("(f p) t -> p f t", p=128), writes=[B["dqd"]])
                        mk.dma(B["kf"][:, :, 0:w], RKF[dr, h * 256:(h + 1) * 256, t0:t0 + w].rearrange("(f p) t -> p f t", p=128), writes=[B["dkf"]])
                        rng = []
                        for j in range(nst):
                            lo, hi = (j * 128, w) if dr == 0 else (0, (j + 1) * 128)
                            rng.append((lo, hi))
                            ps_, dps_ = mk.ps()
                            for half in range(2):
                                MM(ps_[:, lo:hi], dps_, B["kf"][:, half, j * 128:(j + 1) * 128], B["qd"][:, half, lo:hi],
                                   half == 0, half == 1, [B["dkf"], B["dqd"]])
                            dg = slice(j * 128, (j + 1) * 128)
                            I("dve", lambda e: e.tensor_tensor(B["am"][:, j, dg], ps_[:, dg], TRI[dr], ALU.mult),
                              reads=[dps_, dcm], writes=[B["dam"]], add=(j > 0))
                            rl, rh = ((j + 1) * 128, w) if dr == 0 else (0, j * 128)
                            if rh > rl:
                                I("dve", lambda e: e.tensor_copy(B["am"][:, j, rl:rh], ps_[:, rl:rh]), reads=[dps_], writes=[B["dam"]], add=True)
                        for ec in range(4):
                            po, dpo = mk.ps()
                            es_ = slice(ec * 128, (ec + 1) * 128)
                            MM(po[:, 0:w], dpo, Sb[:, sidx, 0, es_], B["qd"][:, 0, 0:w], True, False, [dSb[sidx], B["dqd"]])
                            MM(po[:, 0:w], dpo, Sb[:, sidx, 1, es_], B["qd"][:, 1, 0:w], False, False, [dSb[sidx], B["dqd"]])
                            for j in range(nst):
                                lo, hi = rng[j]
                                MM(po[:, lo:hi], dpo, B["v"][:, j, es_], B["am"][:, j, lo:hi], False, j == nst - 1, [B["dv"], B["dam"]])
                            if ec % 2 == 0:
                                I("act", lambda e: e.copy(B["ob"][:, ec, 0:w], po[:, 0:w]), reads=[dpo], writes=[B["dob"]], add=(ec > 0))
                            else:
                                I("dve", lambda e: e.tensor_copy(B["ob"][:, ec, 0:w], po[:, 0:w]), reads=[dpo], writes=[B["dob"]], add=True)
                        mk.dma(RO[dr, h * 512:(h + 1) * 512, t0:t0 + w].rearrange("(e p) t -> p e t", p=128), B["ob"][:, :, 0:w], reads=[B["dob"]])
                    gC = math.exp(LG[dr][h] * w)
                    for half in range(2):
                        pu_, dpu_ = mk.ps()
                        for j in range(nst):
                            MM(pu_[:, 0:512], dpu_, B["kt"][:, j, half * 128:(half + 1) * 128], B["v"][:, j, :], j == 0, j == nst - 1, [B["dkt"], B["dv"]])
                        I("dve", lambda e: e.tensor_tensor(SR[:, sidx, half, :], SA[:, sidx, half, :], pu_[:, 0:512], ALU.add),
                          reads=[dSA[sidx], dpu_], writes=[dSR[sidx]], add=(half > 0))
                    I("pool", lambda e: e.tensor_scalar(SA[:, sidx], SR[:, sidx], gC, None, ALU.mult), reads=[dSR[sidx]], writes=[dSA[sidx]])
                    I("act", lambda e: e.activation(out=Sb[:, sidx], in_=SR[:, sidx], func=AF.Identity, scale=gC), reads=[dSR[sidx]], writes=[dSb[sidx]])
        mk.barrier()
    if upto == "F":
        return finish(nc, mk, top)

    with ExitStack() as es:
        W1o, dW1o = load_w(es, "W1o", odd_w_out, 2048, D)
        xt = mk.sb(es, "xt", [128, KC, 512], F32); dxt = Dep()
        yb = mk.sb(es, "yb", [128, 16, 512], BF16); dyb = Dep()
        ro = mk.sb(es, "ro", [128, 2, 4, 512], F32); dro = Dep()
        rg = mk.sb(es, "rg", [128, 4, 512], F32); drg = Dep()
        bs = mk.sb(es, "bs", [128, 4, 512], F32); dbs = Dep()
        sqb = mk.sb(es, "sqb", [128, 4, 512], BF16); dsqb = Dep()
        rs_ = mk.sb(es, "rs_", [128, 512], F32); drs_ = Dep()
        for (t0, w) in (TILES[1:] if "G1" not in skip else []):
            mk.dma(xt[:, :, 0:w], X1[:, t0:t0 + w].rearrange("(k p) t -> p k t", p=128), writes=[dxt])
            for h in range(4):
                for dr in range(2):
                    mk.dma(ro[:, dr, :, 0:w], RO[dr, h * 512:(h + 1) * 512, t0:t0 + w].rearrange("(e p) t -> p e t", p=128), writes=[dro], add=(dr > 0))
                mk.dma(rg[:, :, 0:w], RG[h * 512:(h + 1) * 512, t0:t0 + w].rearrange("(e p) t -> p e t", p=128), writes=[drg])
                I("dve", lambda e: e.tensor_tensor(bs[:, :, 0:w], ro[:, 0, :, 0:w], ro[:, 1, :, 0:w], ALU.add), reads=[dro], writes=[dbs])
                I("act", lambda e: e.activation(out=sqb[:, :, 0:w], in_=bs[:, :, 0:w], func=AF.Square), reads=[dbs], writes=[dsqb])
                ps_, dps_ = mk.ps()
                for ec in range(4):
                    MM(ps_[:, 0:w], dps_, ones[:, 2, :], sqb[:, ec, 0:w], ec == 0, ec == 3, [dsqb, dones])
                I("act", lambda e: e.activation(out=rs_[:, 0:w], in_=ps_[:, 0:w], func=AF.Ln, bias=epsc[:, 0:1], scale=1.0), reads=[dps_, deps_], writes=[drs_])
                I("act", lambda e: e.activation(out=rs_[:, 0:w], in_=rs_[:, 0:w], func=AF.Exp, scale=-0.5), reads=[drs_], writes=[drs_])
                I("pool", lambda e: e.tensor_tensor(bs[:, :, 0:w], bs[:, :, 0:w], rs_[:, 0:w].unsqueeze(1).to_broadcast([128, 4, w]), ALU.mult),
                  reads=[dbs, drs_], writes=[dbs])
                I("dve", lambda e: e.tensor_tensor(yb[:, h * 4:(h + 1) * 4, 0:w], bs[:, :, 0:w], rg[:, :, 0:w], ALU.mult),
                  reads=[dbs, drg], writes=[dyb], add=(h > 0))
            for fc in range(KC):
                ps_, dps_ = mk.ps()
                for ic in range(16):
                    MM(ps_[:, 0:w], dps_, W1o[:, ic, fc * 128:(fc + 1) * 128], yb[:, ic, 0:w], ic == 0, ic == 15, [dW1o, dyb])
                I("dve", lambda e: e.scalar_tensor_tensor(xt[:, fc, 0:w], ps_[:, 0:w], MOD[:, 1, 2, fc, 0:1], xt[:, fc, 0:w], ALU.mult, ALU.add),
                  reads=[dps_, dMOD, dxt], writes=[dxt])
            mk.dma(XB[:, t0:t0 + w].rearrange("(k p) t -> p k t", p=128), xt[:, :, 0:w], reads=[dxt])
        mk.barrier()
    if upto == "G1":
        return finish(nc, mk, top)

    ffn_phase(1, XB, outT, True)

    return finish(nc, mk, top)


def finish(nc, mk, top):
    mk.barrier(engines=("sp",))
    top.close()
    return nc


def _const_tables():
    f32 = np.float32
    t = np.arange(L)
    row = (t // 64).astype(f32)
    col = (t % 64).astype(f32)
    inv = (f32(10000.0) ** (-np.arange(16, dtype=f32) / f32(16))).astype(f32)
    ang = np.concatenate([row[:, None] * inv, col[:, None] * inv], axis=-1).astype(f32)
    rope0 = np.zeros((T, 64), f32)
    rope0[:LC, :32] = 1.0
    rope0[LC:, :32] = np.cos(ang)
    rope0[LC:, 32:] = np.sin(ang)
    theta = (1.0 / (f32(10000.0) ** np.linspace(0.0, 1.0, 128, dtype=f32))).astype(f32)
    ang1 = (np.arange(L, dtype=f32)[:, None] * theta).astype(f32)
    rc1 = np.ones((128, T), f32)
    rs1 = np.zeros((128, T), f32)
    rc1[:, LC:] = np.cos(ang1).T
    rs1[:, LC:] = np.sin(ang1).T
    lg_fw = np.log(1.0 - 2.0 ** (-5.0 - np.arange(4, dtype=np.float64)))
    lg = [lg_fw, lg_fw[::-1]]
    gtab = np.zeros((16, 128, 512), f32)
    p = np.arange(512, dtype=np.float64)
    for h in range(4):
        for dr in range(2):
            i = p + 1.0 if dr == 0 else 512.0 - p
            gtab[(h * 2 + dr) * 2 + 0] = np.exp(lg[dr][h] * i)[None, :]
            gtab[(h * 2 + dr) * 2 + 1] = (np.exp(-lg[dr][h] * i) / 16.0)[None, :]
    cmask = np.zeros((128, 6, 512), f32)
    j = np.arange(128)[:, None]
    i = np.arange(128)[None, :]
    cmask[:, 0, :] = np.tile((j >= i).astype(f32), (1, 4))
    cmask[:, 1, :] = np.tile((j <= i).astype(f32), (1, 4))
    cmask[:, 2, :128] = (j <= i).astype(f32)
    cmask[:, 3, :128] = (j >= i).astype(f32)
    cmask[:, 4, :128] = np.eye(128, dtype=f32)
    sm = np.ones(512, f32)
    sm[::64] = 0.0
    cmask[:, 5, :] = sm[None, :]
    return rope0, rc1, rs1, gtab, cmask, lg


def _col(v):
    return np.ascontiguousarray(np.asarray(v, np.float32).reshape(-1, 128).T)


def make_in_maps(x, c, ctx, c_ctx, mod_w, mod_b, norm_g, ffn_w_in, ffn_w_out, even_w_in, even_w_out,
                 attn_qk_norm_g, attn_sink, hgrn_out_norm_g, hgrn_lb, odd_w_in, odd_w_out, cores=range(8)):
    f32 = np.float32
    A = lambda a: np.ascontiguousarray(np.asarray(a, dtype=f32))
    rope0, rc1, rs1, gtab, cmask, _ = _const_tables()
    x, c, ctx, c_ctx = A(x), A(c), A(ctx), A(c_ctx)
    mod_b, norm_g = A(mod_b), A(norm_g)
    shared = {
        "mod_w": A(mod_w),
        "modb": np.ascontiguousarray(np.stack([_col(mod_b[l]) for l in range(2)], axis=1)),
        "ngc": np.ascontiguousarray(np.stack([np.stack([_col(norm_g[l, i]) for i in range(2)], axis=1)
                                              for l in range(2)], axis=1)),
        "ffn_w_in": A(ffn_w_in), "ffn_w_out": A(ffn_w_out),
        "even_w_in": A(even_w_in)[0], "even_w_out": A(even_w_out)[0],
        "odd_w_in": A(odd_w_in)[0], "odd_w_out": A(odd_w_out)[0],
        "gqk": np.ascontiguousarray(np.broadcast_to(np.concatenate(
            [np.tile(A(attn_qk_norm_g)[0, 0], 8), np.tile(A(attn_qk_norm_g)[0, 1], 2)])[None, :], (128, 640))),
        "sink": A(attn_sink).reshape(1, 8),
        "outg": A(hgrn_out_norm_g).reshape(128, 1),
        "lbraw": np.ascontiguousarray(A(hgrn_lb).reshape(2, 4, 128).transpose(2, 0, 1)),
        "rope0": rope0, "rc1": rc1, "rs1": rs1, "gtab": gtab, "cmask": cmask,
    }
    maps = []
    for b in cores:
        m = dict(shared)
        m["xT"] = np.ascontiguousarray(np.concatenate([ctx[b], x[b]], axis=0).T)
        m["ccol"] = np.ascontiguousarray(np.stack([_col(c[b]), _col(c_ctx)], axis=2))
        maps.append(m)
    return maps


_NC_CACHE = {}


def kernel(**inputs):
    if "nc" not in _NC_CACHE:
        _NC_CACHE["nc"] = build_program()
    nc = _NC_CACHE["nc"]
    maps = make_in_maps(**inputs)
    res = run_bass_kernel_spmd(nc, maps, core_ids=list(range(8)))
    out = np.stack([np.ascontiguousarray(r["outT"].T) for r in res.results], axis=0)
    return out.astype(np.float32)
```

```python
import numpy as np
from contextlib import ExitStack
import concourse.bass as bass
import concourse.mybir as mybir
from concourse.bass_utils import run_bass_kernel_spmd

F32 = mybir.dt.float32
BF16 = mybir.dt.bfloat16
AF = mybir.ActivationFunctionType
ALU = mybir.AluOpType
AX = mybir.AxisListType

D = 1024
KC = 8
L = 8192
LC = 256
T = L + LC
NB = T // 128
TILES = [(0, 256)] + [(256 + 512 * i, 512) for i in range(16)]
FH = 2816
EPS = 1e-6


class Dep:
    __slots__ = ("w", "r")

    def __init__(self):
        self.w = {}
        self.r = {}


class Eng:
    def __init__(self, name, e, sem):
        self.name, self.e, self.sem = name, e, sem
        self.count = 0
        self.seen = {}
        self.pending = False


class DSem:
    def __init__(self, name, sem):
        self.name, self.sem = name, sem
        self.count = 0


class MK:
    NDMA = 32

    def __init__(self, nc, es):
        self.nc = nc
        self.E = {}
        for n, a in [("pe", "tensor"), ("act", "scalar"), ("dve", "vector"), ("pool", "gpsimd"), ("sp", "sync")]:
            self.E[n] = Eng(n, getattr(nc, a), es.enter_context(nc.semaphore("s_" + n)))
        self.semof = {n: e.sem for n, e in self.E.items()}
        self.dsems = []
        for i in range(self.NDMA):
            d = DSem("d%d" % i, es.enter_context(nc.semaphore("s_d%d" % i)))
            self.dsems.append(d)
            self.semof[d.name] = d.sem
        self.dnext = 0
        self.psum = []
        self.psn = 0
        self.ninstr = 0

    def _wait(self, eng, key, val):
        if val <= 0 or eng.seen.get(key, 0) >= val:
            return
        eng.e.wait_ge(self.semof[key], val)
        eng.seen[key] = val
        self.ninstr += 1

    def _deps(self, eng, reads, writes):
        need = {}
        for b in reads:
            for k, v in b.w.items():
                if need.get(k, 0) < v:
                    need[k] = v
        for b in writes:
            for dct in (b.w, b.r):
                for k, v in dct.items():
                    if k == eng.name:
                        continue
                    if need.get(k, 0) < v:
                        need[k] = v
        for k, v in need.items():
            self._wait(eng, k, v)

    def _record(self, key, val, reads, writes, add):
        for b in reads:
            if b.r.get(key, 0) < val:
                b.r[key] = val
        for b in writes:
            if add:
                if b.w.get(key, 0) < val:
                    b.w[key] = val
            else:
                b.w = {key: val}
                b.r = {}

    def I(self, en, emit, reads=(), writes=(), inc=True, add=False):
        if en == "pool":
            en = "dve"
        eng = self.E[en]
        self._deps(eng, reads, writes)
        ins = emit(eng.e)
        self.ninstr += 1
        if inc:
            eng.count += 1
            ins.then_inc(eng.sem, 1)
            val = eng.count
            eng.pending = False
        else:
            val = eng.count + 1
            eng.pending = True
        self._record(eng.name, val, reads, writes, add)
        return ins

    def dma(self, out, in_, reads=(), writes=(), q="sp", add=False, **kw):
        eng = self.E[q]
        d = self.dsems[self.dnext]
        self.dnext = (self.dnext + 1) % self.NDMA
        self._wait(eng, d.name, d.count)
        self._deps(eng, reads, writes)
        ins = eng.e.dma_start(out=out, in_=in_, **kw)
        self.ninstr += 1
        d.count += 16
        ins.then_inc(d.sem, 16)
        self._record(d.name, d.count, reads, writes, add)
        return ins

    def barrier(self, engines=("pe", "act", "dve", "pool", "sp")):
        for n, eng in self.E.items():
            assert not eng.pending, n
        for n in engines:
            eng = self.E[n]
            for m, other in self.E.items():
                if m != n:
                    self._wait(eng, m, other.count)
            for d in self.dsems:
                self._wait(eng, d.name, d.count)

    def sb(self, es, name, shape, dt):
        self.nsb = getattr(self, "nsb", 0) + 1
        return es.enter_context(self.nc.sbuf_tensor("%s_%d" % (name, self.nsb), list(shape), dt))

    def init_psum(self, es):
        self.psum = []
        for i in range(8):
            t = es.enter_context(self.nc.psum_tensor("ps%d" % i, [128, 512], F32))
            self.psum.append((t, Dep()))
        self.psn = 0

    def ps(self):
        t = self.psum[self.psn]
        self.psn = (self.psn + 1) % 8
        return t


def eval_blocks():
    import os
    v = os.environ.get("MK_BLOCKS")
    return range(NB) if not v else [int(a) for a in v.split(",")]


def build_program(debug=(), upto="all", skip=()):
    nc = bass.Bass("TRN2", target_bir_lowering=False)

    def din(name, shape):
        return nc.dram_tensor(name, list(shape), F32, kind="ExternalInput").ap()

    def scratch(name, shape, dt):
        if name in debug:
            return nc.dram_tensor(name, list(shape), dt, kind="ExternalOutput").ap()
        return nc.dram_tensor(name, list(shape), dt).ap()

    xT = din("xT", [D, T])
    ccol = din("ccol", [128, KC, 2])
    mod_w = din("mod_w", [2, D, 6 * D])
    modb = din("modb", [128, 2, 48])
    ngc = din("ngc", [128, 2, 2, KC])
    ffn_w_in = din("ffn_w_in", [2, D, 2 * FH])
    ffn_w_out = din("ffn_w_out", [2, FH, D])
    even_w_in = din("even_w_in", [D, 3328])
    even_w_out = din("even_w_out", [D, D])
    odd_w_in = din("odd_w_in", [D, 6144])
    odd_w_out = din("odd_w_out", [2048, D])
    gqk = din("gqk", [128, 640])
    sink = din("sink", [1, 8])
    outg = din("outg", [128, 1])
    lbraw = din("lbraw", [128, 2, 4])
    rope0 = din("rope0", [T, 64])
    rc1 = din("rc1", [128, T])
    rs1 = din("rs1", [128, T])
    gtab = din("gtab", [16, 128, 512])
    cmask = din("cmask", [128, 6, 512])
    outT = nc.dram_tensor("outT", [D, L], F32, kind="ExternalOutput").ap()

    QT0 = scratch("QT0", [512, T], BF16)
    KT0 = scratch("KT0", [128, T], BF16)
    V0 = scratch("V0", [T, 128], BF16)
    QD = scratch("QD", [2, 512, T], BF16)
    KDF = scratch("KDF", [2, 512, T], BF16)
    KDT = scratch("KDT", [2, T, 512], BF16)
    A1 = scratch("A1", [2, 512, T // 64], F32)
    VI = scratch("VI", [T, 512], BF16)
    GS = scratch("GS", [512, T], F32)
    AT = scratch("AT", [512, T], BF16)
    OFW = scratch("OFW", [2, 512, T], F32)
    XA = scratch("XA", [D, T], F32)
    X1 = scratch("X1", [D, T], F32)
    RQD = scratch("RQD", [2, 1024, T], BF16)
    RKF = scratch("RKF", [2, 1024, T], BF16)
    RKT = scratch("RKT", [2, T, 1024], BF16)
    RV = scratch("RV", [T, 2048], BF16)
    RG = scratch("RG", [2048, T], F32)
    RO = scratch("RO", [2, 2048, T], F32)
    XB = scratch("XB", [D, T], F32)

    top = ExitStack()
    mk = MK(nc, top)
    mk.init_psum(top)
    I = mk.I

    MOD = mk.sb(top, "MOD", [128, 2, 6, KC, 2], F32); dMOD = Dep()
    GM = mk.sb(top, "GM", [128, 2, 2, KC, 2], F32); dGM = Dep()
    cm = mk.sb(top, "cm", [128, 6, 512], BF16); dcm = Dep()
    ones = mk.sb(top, "ones", [128, 3, 128], BF16); dones = Dep()
    epsc = mk.sb(top, "epsc", [128, 1], F32); deps_ = Dep()
    ngt = mk.sb(top, "ngt", [128, 2, 2, KC], F32); dng = Dep()
    mk.dma(cm[:], cmask[:, :, :], writes=[dcm], q="pool")
    mk.dma(ngt[:], ngc[:, :, :, :], writes=[dng])
    I("dve", lambda e: e.memset(ones[:, 0, :], 1.0 / 1024), writes=[dones])
    I("dve", lambda e: e.memset(ones[:, 1, :], 1.0 / 128), writes=[dones], add=True)
    I("dve", lambda e: e.memset(ones[:, 2, :], 1.0 / 512), writes=[dones], add=True)
    I("dve", lambda e: e.memset(epsc[:], EPS), writes=[deps_])
    ident = cm[:, 4, 0:128]
    MPREV, MNEXT = cm[:, 0, :], cm[:, 1, :]
    TRI = [cm[:, 2, 0:128], cm[:, 3, 0:128]]
    SCANM = None

    def MM(out, dps, lhsT, rhs, st, sp, rd):
        I("pe", lambda e: e.matmul(out, lhsT, rhs, start=st, stop=sp), reads=rd, writes=[dps], inc=sp, add=not st)

    def load_w(es, name, src, rows, cols, c0=0):
        n = rows // 128
        t = mk.sb(es, name, [128, n, cols], BF16)
        d = Dep()
        for k in range(n):
            mk.dma(t[:, k, :], src[k * 128:(k + 1) * 128, c0:c0 + cols], writes=[d], q="pool", add=(k > 0))
        return t, d

    with ExitStack() as es:
        cc = mk.sb(es, "cc", [128, KC, 2], F32); dcc = Dep()
        ccb = mk.sb(es, "ccb", [128, KC, 2], BF16); dccb = Dep()
        mbt = mk.sb(es, "mbt", [128, 2, 48], F32); dmb = Dep()
        mk.dma(cc[:], ccol[:, :, :], writes=[dcc])
        mk.dma(mbt[:], modb[:, :, :], writes=[dmb])
        I("act", lambda e: e.activation(out=ccb[:], in_=cc[:], func=AF.Silu), reads=[dcc], writes=[dccb])
        for l in range(2):
            for half in range(2):
                with ExitStack() as es2:
                    wm, dwm = load_w(es2, "wm", mod_w[l], D, 3072, c0=half * 3072)
                    ps, dps = mk.ps()
                    for fc in range(24):
                        for kc in range(KC):
                            MM(ps[:, fc * 2:fc * 2 + 2], dps, wm[:, kc, fc * 128:(fc + 1) * 128], ccb[:, kc, :],
                               kc == 0, kc == KC - 1, [dwm, dccb])
                    mv = MOD[:, l].rearrange("p j k w -> p (j k) w")[:, half * 24:(half + 1) * 24, :]
                    I("dve", lambda e: e.tensor_tensor(
                        mv, ps[:, 0:48].rearrange("p (f w) -> p f w", w=2),
                        mbt[:, l, half * 24:(half + 1) * 24].unsqueeze(2).to_broadcast([128, 24, 2]), ALU.add),
                      reads=[dps, dmb], writes=[dMOD], add=True)
                    mk.barrier()
        for l in range(2):
            for i, j in ((0, 1), (1, 4)):
                I("dve", lambda e: e.tensor_scalar(GM[:, l, i], MOD[:, l, j], 1.0, None, ALU.add),
                  reads=[dMOD], writes=[dGM], add=True)
                I("dve", lambda e: e.tensor_tensor(GM[:, l, i], GM[:, l, i],
                                                   ngt[:, l, i].unsqueeze(2).to_broadcast([128, KC, 2]), ALU.mult),
                  reads=[dGM, dng], writes=[dGM], add=True)
        mk.barrier()

    def norm_mod(xt, dxt, hb, dhb, sq, dsq, rst, drst, l, i, which, w):
        I("act", lambda e: e.activation(out=sq[:, :, 0:w], in_=xt[:, :, 0:w], func=AF.Square), reads=[dxt], writes=[dsq])
        ps, dps = mk.ps()
        for kc in range(KC):
            MM(ps[:, 0:w], dps, ones[:, 0, :], sq[:, kc, 0:w], kc == 0, kc == KC - 1, [dsq, dones])
        I("act", lambda e: e.activation(out=rst[:, 0:w], in_=ps[:, 0:w], func=AF.Ln, bias=epsc[:, 0:1], scale=1.0),
          reads=[dps, deps_], writes=[drst])
        I("act", lambda e: e.activation(out=rst[:, 0:w], in_=rst[:, 0:w], func=AF.Exp, scale=-0.5),
          reads=[drst], writes=[drst])
        sj = 0 if i == 0 else 3
        for kc in range(KC):
            en = "dve" if kc % 2 == 0 else "pool"
            I(en, lambda e: e.tensor_tensor(xn_[:, kc % 2, 0:w], xt[:, kc, 0:w], rst[:, 0:w], ALU.mult),
              reads=[dxt, drst], writes=[dxn_[kc % 2]])
            I("act", lambda e: e.activation(out=hb[:, kc, 0:w], in_=xn_[:, kc % 2, 0:w], func=AF.Identity,
                                            scale=GM[:, l, i, kc, which:which + 1],
                                            bias=MOD[:, l, sj, kc, which:which + 1]),
              reads=[dxn_[kc % 2], dGM, dMOD], writes=[dhb], add=(kc > 0))

    with ExitStack() as es:
        W0, dW0 = load_w(es, "W0", even_w_in, D, 3328)
        xt = mk.sb(es, "xt", [128, KC, 512], F32); dxt = Dep()
        xn_ = mk.sb(es, "xn", [128, 2, 512], F32); dxn_ = [Dep(), Dep()]
        sq = mk.sb(es, "sq", [128, KC, 512], BF16); dsq = Dep()
        rst = mk.sb(es, "rst", [128, 512], F32); drst = Dep()
        hb = mk.sb(es, "hb", [128, KC, 512], BF16); dhb = Dep()
        gq = mk.sb(es, "gq", [128, 640], F32); dgq = Dep()
        mk.dma(gq[:], gqk[:, :], writes=[dgq])
        lbr = mk.sb(es, "lbr", [128, 2, 4], F32); dlbr = Dep()
        LB = mk.sb(es, "LB", [128, 3, 4], F32); dLB = Dep()
        mk.dma(lbr[:], lbraw[:, :, :], writes=[dlbr])
        I("dve", lambda e: e.tensor_tensor(LB[:, 1, :], lbr[:, 0, :], lbr[:, 1, :], ALU.subtract), reads=[dlbr], writes=[dLB])
        I("act", lambda e: e.activation(out=LB[:, 0, :], in_=LB[:, 1, :], func=AF.Sigmoid), reads=[dLB], writes=[dLB], add=True)
        I("dve", lambda e: e.tensor_scalar(LB[:, 1, :], LB[:, 0, :], -1.0, 1.0, ALU.mult, ALU.add), reads=[dLB], writes=[dLB], add=True)
        I("dve", lambda e: e.tensor_scalar(LB[:, 2, :], LB[:, 0, :], -1.0, None, ALU.add), reads=[dLB], writes=[dLB], add=True)
        scm = mk.sb(es, "scm", [128, 512], F32); dscm = Dep()
        I("dve", lambda e: e.tensor_copy(scm[:], cm[:, 5, :]), reads=[dcm], writes=[dscm])
        rp = mk.sb(es, "rp", [128, 4, 64], F32); drp = Dep()
        sqa = mk.sb(es, "sqa", [128, 640], F32); dsqa = Dep()
        ssq = mk.sb(es, "ssq", [128, 10], F32); dssq = Dep()
        qn = mk.sb(es, "qn", [128, 640], F32); dqn = Dep()
        t1 = mk.sb(es, "t1", [128, 10, 32], F32); dt1 = Dep()
        t2 = mk.sb(es, "t2", [128, 10, 32], F32); dt2 = Dep()
        qr = mk.sb(es, "qr", [128, 640], BF16); dqr = Dep()
        qTs = mk.sb(es, "qTs", [128, 5, 512], BF16); dqTs = Dep()
        vst = mk.sb(es, "vst", [128, 4, 128], BF16); dvst = Dep()
        vis = mk.sb(es, "vis", [128, 4, 512], BF16); dvis = Dep()
        qs = mk.sb(es, "qs", [128, 4, 512], F32); dqs = [Dep() for _ in range(4)]
        rf = mk.sb(es, "rf", [128, 4, 2, 512], F32); drf = [Dep() for _ in range(4)]
        gsb = mk.sb(es, "gsb", [128, 2, 512], F32); dgsb = [Dep(), Dep()]
        lfp_ = mk.sb(es, "lfp", [128, 4, 520], F32); dlf_ = [Dep() for _ in range(4)]
        kk_ = mk.sb(es, "kk", [128, 4, 512], F32); dkk_ = [Dep() for _ in range(4)]
        cu_ = mk.sb(es, "cu", [128, 4, 512], F32); dcu_ = [Dep() for _ in range(4)]
        ee_ = mk.sb(es, "ee", [128, 4, 2, 512], F32); dee_ = [Dep() for _ in range(4)]
        qdb_ = mk.sb(es, "qdb", [128, 4, 512], BF16); dqdb_ = [Dep() for _ in range(4)]
        kdb_ = mk.sb(es, "kdb", [128, 4, 512], BF16); dkdb_ = [Dep() for _ in range(4)]
        kts_ = mk.sb(es, "kts", [128, 2, 2, 4, 128], BF16); dkts_ = [Dep(), Dep()]
        a1s_ = mk.sb(es, "a1s", [128, 2, 2, 8], F32); da1s_ = [Dep(), Dep()]
        a1t_ = mk.sb(es, "a1t", [128, 2, 8], F32); da1t_ = [Dep(), Dep()]
        I("dve", lambda e: e.memset(lfp_[:], 0.0), writes=dlf_)

        for (t0, w) in (TILES if "A" not in skip else []):
            which = 1 if t0 == 0 else 0
            nst = w // 128
            nch = w // 64
            mk.dma(xt[:, :, 0:w], xT[:, t0:t0 + w].rearrange("(k p) t -> p k t", p=128), writes=[dxt])
            norm_mod(xt, dxt, hb, dhb, sq, dsq, rst, drst, 0, 0, which, w)
            for st in range(nst):
                tk = slice(st * 128, (st + 1) * 128)
                pq, dpq = mk.ps()
                pk, dpk = mk.ps()
                for kc in range(KC):
                    MM(pq[:, 0:512], dpq, hb[:, kc, tk], W0[:, kc, 0:512], kc == 0, kc == KC - 1, [dhb, dW0])
                for kc in range(KC):
                    MM(pk[:, 0:256], dpk, hb[:, kc, tk], W0[:, kc, 512:768], kc == 0, kc == KC - 1, [dhb, dW0])
                mk.dma(rp[:, st, :], rope0[t0 + st * 128:t0 + (st + 1) * 128, :], writes=[drp], add=True)
                I("act", lambda e: e.activation(out=sqa[:, 0:512], in_=pq[:, 0:512], func=AF.Square), reads=[dpq], writes=[dsqa])
                I("act", lambda e: e.activation(out=sqa[:, 512:640], in_=pk[:, 0:128], func=AF.Square), reads=[dpk], writes=[dsqa], add=True)
                I("dve", lambda e: e.tensor_reduce(ssq[:], sqa[:].rearrange("p (h d) -> p h d", d=64), AX.X, ALU.add),
                  reads=[dsqa], writes=[dssq])
                I("act", lambda e: e.activation(out=ssq[:], in_=ssq[:], func=AF.Ln, bias=epsc[:, 0:1], scale=1.0 / 64),
                  reads=[dssq, deps_], writes=[dssq])
                I("act", lambda e: e.activation(out=ssq[:], in_=ssq[:], func=AF.Exp, scale=-0.5), reads=[dssq], writes=[dssq])
                I("dve", lambda e: e.tensor_tensor(qn[:, 0:512].rearrange("p (h d) -> p h d", d=64),
                                                   pq[:, 0:512].rearrange("p (h d) -> p h d", d=64),
                                                   ssq[:, 0:8].unsqueeze(2).to_broadcast([128, 8, 64]), ALU.mult),
                  reads=[dpq, dssq], writes=[dqn])
                I("dve", lambda e: e.tensor_tensor(qn[:, 512:640].rearrange("p (h d) -> p h d", d=64),
                                                   pk[:, 0:128].rearrange("p (h d) -> p h d", d=64),
                                                   ssq[:, 8:10].unsqueeze(2).to_broadcast([128, 2, 64]), ALU.mult),
                  reads=[dpk, dssq], writes=[dqn], add=True)
                I("dve", lambda e: e.tensor_copy(vst[:, st, :], pk[:, 128:256]), reads=[dpk], writes=[dvst], add=True)
                I("pool", lambda e: e.tensor_tensor(qn[:], qn[:], gq[:], ALU.mult), reads=[dqn, dgq], writes=[dqn])
                qv = qn[:].rearrange("p (h d) -> p h d", d=64)
                qrv = qr[:].rearrange("p (h d) -> p h d", d=64)
                cosb = rp[:, st, 0:32].unsqueeze(1).to_broadcast([128, 10, 32])
                sinb = rp[:, st, 32:64].unsqueeze(1).to_broadcast([128, 10, 32])
                x1, x2 = qv[:, :, 0:32], qv[:, :, 32:64]
                I("dve", lambda e: e.tensor_tensor(t1[:], x1, cosb, ALU.mult), reads=[dqn, drp], writes=[dt1])
                I("pool", lambda e: e.tensor_tensor(t2[:], x2, sinb, ALU.mult), reads=[dqn, drp], writes=[dt2])
                I("dve", lambda e: e.tensor_tensor(qrv[:, :, 0:32], t1[:], t2[:], ALU.subtract), reads=[dt1, dt2], writes=[dqr])
                I("pool", lambda e: e.tensor_tensor(t1[:], x2, cosb, ALU.mult), reads=[dqn, drp, dqr], writes=[dt1])
                I("dve", lambda e: e.tensor_tensor(t2[:], x1, sinb, ALU.mult), reads=[dqn, drp, dqr], writes=[dt2])
                I("pool", lambda e: e.tensor_tensor(qrv[:, :, 32:64], t1[:], t2[:], ALU.add), reads=[dt1, dt2], writes=[dqr], add=True)
                pt, dpt = mk.ps()
                ptb = pt[:].bitcast(BF16)
                for j in range(5):
                    I("pe", lambda e: e.transpose(ptb[:, j * 128:(j + 1) * 128], qr[:, j * 128:(j + 1) * 128], ident),
                      reads=[dqr, dcm], writes=[dpt], inc=(j == 4), add=(j > 0))
                I("act", lambda e: e.copy(qTs[:, :, tk], ptb[:, 0:640].rearrange("p (j t) -> p j t", t=128)),
                  reads=[dpt], writes=[dqTs], add=True)
                pv, dpv = mk.ps()
                for kc in range(KC):
                    MM(pv[:, 0:512], dpv, hb[:, kc, tk], W0[:, kc, 2304:2816], kc == 0, kc == KC - 1, [dhb, dW0])
                I("dve", lambda e: e.tensor_copy(vis[:, st, :], pv[:, 0:512]), reads=[dpv], writes=[dvis], add=True)
            mk.dma(QT0[:, t0:t0 + w].rearrange("(j p) t -> p j t", p=128), qTs[:, 0:4, 0:w], reads=[dqTs])
            mk.dma(KT0[:, t0:t0 + w], qTs[:, 4, 0:w], reads=[dqTs])
            mk.dma(V0[t0:t0 + w, :].rearrange("(s p) c -> p s c", p=128), vst[:, 0:nst, :], reads=[dvst])
            mk.dma(VI[t0:t0 + w, :].rearrange("(s p) c -> p s c", p=128), vis[:, 0:nst, :], reads=[dvis])
            for h in range(4):
                pss = []
                for base in (768, 2816, 1280, 1792):
                    p_, dp_ = mk.ps()
                    c0 = base + h * 128
                    for kc in range(KC):
                        MM(p_[:, 0:w], dp_, W0[:, kc, c0:c0 + 128], hb[:, kc, 0:w], kc == 0, kc == KC - 1, [dhb, dW0])
                    pss.append((p_, dp_))
                (pqq, dpqq), (pg, dpg), (pff, dpff), (pfb, dpfb) = pss
                gi = h % 2
                I("act", lambda e: e.activation(out=qs[:, h, 0:w], in_=pqq[:, 0:w], func=AF.Sigmoid), reads=[dpqq], writes=[dqs[h]])
                I("dve", lambda e: e.tensor_tensor(qs[:, h, 0:w], qs[:, h, 0:w], pqq[:, 0:w], ALU.mult), reads=[dqs[h], dpqq], writes=[dqs[h]])
                I("act", lambda e: e.activation(out=gsb[:, gi, 0:w], in_=pg[:, 0:w], func=AF.Sigmoid), reads=[dpg], writes=[dgsb[gi]])
                I("dve", lambda e: e.tensor_tensor(gsb[:, gi, 0:w], gsb[:, gi, 0:w], pg[:, 0:w], ALU.mult), reads=[dgsb[gi], dpg], writes=[dgsb[gi]])
                mk.dma(GS[h * 128:(h + 1) * 128, t0:t0 + w], gsb[:, gi, 0:w], reads=[dgsb[gi]])
                I("act", lambda e: e.activation(out=rf[:, h, 0, 0:w], in_=pff[:, 0:w], func=AF.Sigmoid), reads=[dpff], writes=[drf[h]])
                I("act", lambda e: e.activation(out=rf[:, h, 1, 0:w], in_=pfb[:, 0:w], func=AF.Sigmoid), reads=[dpfb], writes=[drf[h]], add=True)
            for h in range(4):
                hp = h % 2
                lfp = lfp_[:, hp * 2:hp * 2 + 2]; dlf = dlf_[hp * 2:hp * 2 + 2]
                kk = kk_[:, hp * 2:hp * 2 + 2]; dkk = dkk_[hp * 2:hp * 2 + 2]
                cu = cu_[:, hp * 2:hp * 2 + 2]; dcu = dcu_[hp * 2:hp * 2 + 2]
                ee = ee_[:, hp * 2:hp * 2 + 2]; dee = dee_[hp * 2:hp * 2 + 2]
                qdb = qdb_[:, hp * 2:hp * 2 + 2]; dqdb = dqdb_[hp * 2:hp * 2 + 2]
                kdb = kdb_[:, hp * 2:hp * 2 + 2]; dkdb = dkdb_[hp * 2:hp * 2 + 2]
                kts = kts_[:, hp]; dkts = dkts_[hp]
                a1s = a1s_[:, hp]; da1s = da1s_[hp]
                a1t = a1t_[:, hp]; da1t = da1t_[hp]
                for dr in range(2):
                    r_ = rf[:, h, dr, 0:w]
                    lf = lfp[:, dr, 1:1 + w]
                    I("act", lambda e: e.activation(out=lf, in_=r_, func=AF.Ln, scale=LB[:, 1, h:h + 1], bias=LB[:, 0, h:h + 1]),
                      reads=[drf[h], dLB], writes=[dlf[dr]])
                    I("pool", lambda e: e.tensor_scalar(kk[:, dr, 0:w], r_, LB[:, 2, h:h + 1], LB[:, 1, h:h + 1], ALU.mult, ALU.add),
                      reads=[drf[h], dLB], writes=[dkk[dr]])
                    if dr == 0:
                        I("dve", lambda e: e.tensor_tensor_scan(cu[:, dr, 0:w], scm[:, 0:w], lf, 0.0, ALU.mult, ALU.add),
                          reads=[dscm, dlf[dr]], writes=[dcu[dr]])
                        a1src = ee[:, dr, 0, 0:w]
                    else:
                        I("dve", lambda e: e.tensor_tensor_scan(cu[:, dr, 0:w], lfp[:, dr, 0:w], scm[:, 0:w], 0.0, ALU.add, ALU.mult),
                          reads=[dscm, dlf[dr]], writes=[dcu[dr]])
                    sgn = (1.0, -1.0) if dr == 0 else (-1.0, 1.0)
                    I("act", lambda e: e.activation(out=ee[:, dr, 0, 0:w], in_=cu[:, dr, 0:w], func=AF.Exp, scale=sgn[0]),
                      reads=[dcu[dr]], writes=[dee[dr]])
                    I("act", lambda e: e.activation(out=ee[:, dr, 1, 0:w], in_=cu[:, dr, 0:w], func=AF.Exp, scale=sgn[1]),
                      reads=[dcu[dr]], writes=[dee[dr]], add=True)
                    I("dve", lambda e: e.tensor_tensor(qdb[:, dr, 0:w], qs[:, h, 0:w], ee[:, dr, 0, 0:w], ALU.mult),
                      reads=[dqs[h], dee[dr]], writes=[dqdb[dr]])
                    I("pool", lambda e: e.tensor_tensor(kdb[:, dr, 0:w], kk[:, dr, 0:w], ee[:, dr, 1, 0:w], ALU.mult),
                      reads=[dkk[dr], dee[dr]], writes=[dkdb[dr]])
                    if dr == 0:
                        I("dve", lambda e: e.tensor_copy(a1s[:, dr, 0:nch], ee[:, dr, 0, 0:w].rearrange("p (c t) -> p c t", t=64)[:, :, 63]),
                          reads=[dee[dr]], writes=[da1s])
                    else:
                        I("dve", lambda e: e.tensor_tensor(a1t[:, 0:nch], cu[:, dr, 0:w].rearrange("p (c t) -> p c t", t=64)[:, :, 63],
                                                           lfp[:, dr, 1:1 + w].rearrange("p (c t) -> p c t", t=64)[:, :, 63], ALU.add),
                          reads=[dcu[dr], dlf[dr]], writes=[da1t])
                        I("act", lambda e: e.activation(out=a1s[:, dr, 0:nch], in_=a1t[:, 0:nch], func=AF.Exp),
                          reads=[da1t], writes=[da1s], add=True)
                    mk.dma(QD[dr, h * 128:(h + 1) * 128, t0:t0 + w], qdb[:, dr, 0:w], reads=[dqdb[dr]])
                    mk.dma(KDF[dr, h * 128:(h + 1) * 128, t0:t0 + w], kdb[:, dr, 0:w], reads=[dkdb[dr]])
                mk.dma(A1[:, h * 128:(h + 1) * 128, t0 // 64:t0 // 64 + nch].rearrange("r p c -> p r c"), a1s[:, :, 0:nch], reads=[da1s])
                pt, dpt = mk.ps()
                ptb = pt[:].bitcast(BF16)
                n = 0
                for dr in range(2):
                    for st in range(nst):
                        I("pe", lambda e: e.transpose(ptb[:, (dr * 4 + st) * 128:(dr * 4 + st + 1) * 128],
                                                      kdb[:, dr, st * 128:(st + 1) * 128], ident),
                          reads=[dkdb[dr], dcm], writes=[dpt], inc=(n == 2 * nst - 1), add=(n > 0))
                        n += 1
                I("act", lambda e: e.copy(kts[:, :, 0:nst, :], ptb[:, 0:1024].rearrange("p (r s d) -> p r s d", r=2, s=4)[:, :, 0:nst, :]),
                  reads=[dpt], writes=[dkts])
                for dr in range(2):
                    mk.dma(KDT[dr, t0:t0 + w, h * 128:(h + 1) * 128].rearrange("(s p) d -> p s d", p=128), kts[:, dr, 0:nst, :], reads=[dkts])
        mk.barrier()
    if upto == "A":
        return finish(nc, mk, top)

    with ExitStack() as es:
        KTr = mk.sb(es, "KTr", [128, T], BF16); dKT = Dep()
        mk.dma(KTr[:], KT0[:, :], writes=[dKT])
        VA = mk.sb(es, "VA", [128, NB, 2, 128], BF16); dVA = Dep()
        I("pool", lambda e: e.memset(VA[:], 1.0), writes=[dVA])
        for b0 in range(0, NB, 22):
            for kv in range(2):
                mk.dma(VA[:, b0:b0 + 22, kv, 0:64],
                       V0[b0 * 128:(b0 + 22) * 128, kv * 64:(kv + 1) * 64].rearrange("(b p) d -> p b d", p=128), writes=[dVA], add=True)
        skr = mk.sb(es, "skr", [1, 8], F32); dsk = Dep()
        srow = mk.sb(es, "srow", [1, 8, 128], BF16); dsrow = Dep()
        sel = mk.sb(es, "sel", [1, 128], BF16); dsel = Dep()
        mk.dma(skr[:], sink[:, :], writes=[dsk])
        I("act", lambda e: e.activation(out=skr[:], in_=skr[:], func=AF.Exp), reads=[dsk], writes=[dsk])
        I("dve", lambda e: e.tensor_copy(srow[:], skr[:].unsqueeze(2).to_broadcast([1, 8, 128])), reads=[dsk], writes=[dsrow])
        I("dve", lambda e: e.memset(sel[:, 0:64], 0.0), writes=[dsel])
        I("dve", lambda e: e.memset(sel[:, 64:128], 1.0), writes=[dsel], add=True)
        qts = [mk.sb(es, "qt", [128, 4, 128], BF16) for _ in range(2)]; dqts = [Dep(), Dep()]
        pts = [mk.sb(es, "pt", [128, 512], BF16) for _ in range(6)]; dpts = [Dep() for _ in range(6)]
        rcs = mk.sb(es, "rc", [64, 2, 512], F32); drc = [Dep(), Dep()]
        aos = [mk.sb(es, "ao", [64, 2, 512], BF16) for _ in range(2)]; daos = [[Dep(), Dep()], [Dep(), Dep()]]
        pn = 0
        for gb in (eval_blocks() if "B" not in skip else []):
            qt, dqt = qts[gb % 2], dqts[gb % 2]
            tb = slice(gb * 128, (gb + 1) * 128)
            for kv in range(2):
                mk.dma(qt[kv * 64:(kv + 1) * 64, :, :],
                       QT0[kv * 256:(kv + 1) * 256, tb].rearrange("(g d) t -> d g t", d=64), writes=[dqt], add=(kv > 0))
            for kv in range(2):
                pr = slice(kv * 64, (kv + 1) * 64)
                if gb < 2:
                    keys = [(0, None), (1, None)]
                else:
                    keys = []
                    if gb > 2:
                        keys.append((gb - 1, MPREV))
                    keys.append((gb, None))
                    if gb < NB - 1:
                        keys.append((gb + 1, MNEXT))
                    keys += [(0, None), (1, None)]
                used = []
                for (kb, msk) in keys:
                    ps_, dps_ = mk.ps()
                    MM(ps_[:, 0:512], dps_, KTr[pr, kb * 128:(kb + 1) * 128], qt[pr, :, :].rearrange("p g t -> p (g t)"),
                       True, True, [dKT, dqt])
                    pt_, dpt_ = pts[pn % 6], dpts[pn % 6]
                    pn += 1
                    I("act", lambda e: e.activation(out=pt_[:], in_=ps_[:, 0:512], func=AF.Exp, scale=0.125), reads=[dps_], writes=[dpt_])
                    if msk is not None:
                        I("pool", lambda e: e.tensor_tensor(pt_[:], pt_[:], msk, ALU.mult), reads=[dpt_, dcm], writes=[dpt_])
                    used.append((kb, pt_, dpt_))
                po, dpo = mk.ps()
                for i_, (kb, pt_, dpt_) in enumerate(used):
                    MM(po[:, 0:512], dpo, VA[:, kb, kv, :], pt_[:], i_ == 0, False, [dVA, dpt_])
                MM(po[:, 0:512], dpo, sel[0:1, :], srow[0:1, kv * 4:(kv + 1) * 4, :].rearrange("p g t -> p (g t)"),
                   False, True, [dsel, dsrow])
                ao, dao = aos[gb % 2], daos[gb % 2][kv]
                I("dve", lambda e: e.reciprocal(rcs[0:64, kv, :], po[64:128, 0:512]), reads=[dpo], writes=[drc[kv]])
                I("dve", lambda e: e.tensor_tensor(ao[0:64, kv, :], po[0:64, 0:512], rcs[0:64, kv, :], ALU.mult),
                  reads=[dpo, drc[kv]], writes=[dao])
                mk.dma(AT[kv * 256:(kv + 1) * 256, tb].rearrange("(g d) t -> d g t", d=64),
                       ao[0:64, kv, :].rearrange("p (g t) -> p g t", t=128), reads=[dao])
        mk.barrier()
    if upto == "B":
        return finish(nc, mk, top)

    with ExitStack() as es:
        SA = mk.sb(es, "SA", [128, 8, 128], F32); SR = mk.sb(es, "SR", [128, 8, 128], F32)
        Sb = mk.sb(es, "Sb", [128, 8, 128], BF16)
        dSA = [Dep() for _ in range(8)]; dSR = [Dep() for _ in range(8)]; dSb = [Dep() for _ in range(8)]
        I("dve", lambda e: e.memset(SA[:], 0.0), writes=dSA)
        I("dve", lambda e: e.memset(Sb[:], 0.0), writes=dSb)
        am = mk.sb(es, "am", [64, 8, 64], BF16); dam = [Dep() for _ in range(8)]
        bufs = {}
        for par in range(2):
            for sidx in range(8):
                bufs[(par, sidx)] = dict(
                    qd=mk.sb(es, "gqd", [128, 512], BF16), kdf=mk.sb(es, "gkf", [128, 512], BF16),
                    kdt=mk.sb(es, "gkt", [64, 8, 128], BF16), v=mk.sb(es, "gv", [64, 8, 128], BF16),
                    a1=mk.sb(es, "ga1", [128, 8], F32), ob=mk.sb(es, "gob", [128, 512], F32),
                    dqd=Dep(), dkdf=Dep(), dkdt=Dep(), dv=Dep(), da1=Dep(), dob=Dep())
        order = [list(range(17)), [0] + list(range(16, 0, -1))]
        for step in (range(17) if "C" not in skip else []):
            par = step % 2
            cur = {}
            for dr in range(2):
                t0, w = TILES[order[dr][step]]
                nch = w // 64
                for h in range(4):
                    sidx = dr * 4 + h
                    B = bufs[(par, sidx)]
                    hs = slice(h * 128, (h + 1) * 128)
                    mk.dma(B["qd"][:, 0:w], QD[dr, hs, t0:t0 + w], writes=[B["dqd"]])
                    mk.dma(B["kdf"][:, 0:w], KDF[dr, hs, t0:t0 + w], writes=[B["dkdf"]])
                    mk.dma(B["kdt"][:, 0:nch, :], KDT[dr, t0:t0 + w, hs].rearrange("(c p) d -> p c d", p=64), writes=[B["dkdt"]])
                    mk.dma(B["v"][:, 0:nch, :], VI[t0:t0 + w, hs].rearrange("(c p) d -> p c d", p=64), writes=[B["dv"]])
                    mk.dma(B["a1"][:, 0:nch], A1[dr, hs, t0 // 64:t0 // 64 + nch], writes=[B["da1"]])
                    cur[sidx] = (B, t0, w, nch)
            for c in range(8):
                for dr in range(2):
                    for h in range(4):
                        sidx = dr * 4 + h
                        B, t0, w, nch = cur[sidx]
                        if c >= nch:
                            continue
                        cc = c if dr == 0 else nch - 1 - c
                        tk = slice(cc * 64, cc * 64 + 64)
                        a1c = B["a1"][:, cc:cc + 1]
                        sa, sr, sb_ = SA[:, sidx, :], SR[:, sidx, :], Sb[:, sidx, :]
                        if dr == 1:
                            I("act", lambda e: e.activation(out=sb_, in_=sa, func=AF.Identity, scale=a1c),
                              reads=[dSA[sidx], B["da1"]], writes=[dSb[sidx]])
                            I("pool", lambda e: e.tensor_scalar(sr, sa, a1c, None, ALU.mult),
                              reads=[dSA[sidx], B["da1"]], writes=[dSR[sidx]])
                        pa, dpa = mk.ps()
                        MM(pa[0:64, 0:64], dpa, B["kdf"][:, tk], B["qd"][:, tk], True, True, [B["dkdf"], B["dqd"]])
                        I("dve", lambda e: e.tensor_tensor(am[:, sidx, :], pa[0:64, 0:64], TRI[dr][0:64, 0:64], ALU.mult),
                          reads=[dpa, dcm], writes=[dam[sidx]])
                        MM(pa[:, 64:128], dpa, B["v"][:, cc, :], am[:, sidx, :], True, False, [B["dv"], dam[sidx]])
                        MM(pa[:, 64:128], dpa, sb_, B["qd"][:, tk], False, True, [dSb[sidx], B["dqd"]])
                        pu_, dpu_ = mk.ps()
                        MM(pu_[:, 0:128], dpu_, B["kdt"][:, cc, :], B["v"][:, cc, :], True, True, [B["dkdt"], B["dv"]])
                        I("act", lambda e: e.copy(B["ob"][:, tk], pa[:, 64:128]), reads=[dpa], writes=[B["dob"]], add=True)
                        if dr == 0:
                            I("dve", lambda e: e.tensor_tensor(sr, sa, pu_[:, 0:128], ALU.add),
                              reads=[dSA[sidx], dpu_], writes=[dSR[sidx]])
                            I("pool", lambda e: e.tensor_scalar(sa, sr, a1c, None, ALU.mult),
                              reads=[dSR[sidx], B["da1"]], writes=[dSA[sidx]])
                            I("act", lambda e: e.activation(out=sb_, in_=sr, func=AF.Identity, scale=a1c),
                              reads=[dSR[sidx], B["da1"]], writes=[dSb[sidx]])
                        else:
                            I("dve", lambda e: e.tensor_tensor(sa, sr, pu_[:, 0:128], ALU.add),
                              reads=[dSR[sidx], dpu_], writes=[dSA[sidx]])
            for sidx in range(8):
                B, t0, w, nch = cur[sidx]
                dr, h = sidx // 4, sidx % 4
                mk.dma(OFW[dr, h * 128:(h + 1) * 128, t0:t0 + w], B["ob"][:, 0:w], reads=[B["dob"]])
        mk.barrier()
    if upto == "C":
        return finish(nc, mk, top)

    with ExitStack() as es:
        Wo, dWo = load_w(es, "Wo", even_w_out, D, D)
        ogt = mk.sb(es, "ogt", [128, 1], F32); dog = Dep()
        mk.dma(ogt[:], outg[:, :], writes=[dog])
        xt = mk.sb(es, "xt", [128, KC, 512], F32); dxt = Dep()
        at = mk.sb(es, "at", [128, 4, 512], BF16); dat = Dep()
        of = mk.sb(es, "of", [128, 2, 4, 512], F32); dof = Dep()
        gs = mk.sb(es, "gs", [128, 4, 512], F32); dgs = Dep()
        bs = mk.sb(es, "bs", [128, 4, 512], F32); dbs = Dep()
        sqb = mk.sb(es, "sqb", [128, 4, 512], BF16); dsqb = Dep()
        rs4 = mk.sb(es, "rs4", [128, 4, 512], F32); drs4 = Dep()
        bl = mk.sb(es, "bl", [128, 4, 512], BF16); dbl = Dep()
        for (t0, w) in (TILES if "D1" not in skip else []):
            which = 1 if t0 == 0 else 0
            mk.dma(xt[:, :, 0:w], xT[:, t0:t0 + w].rearrange("(k p) t -> p k t", p=128), writes=[dxt])
            mk.dma(at[:, :, 0:w], AT[:, t0:t0 + w].rearrange("(k p) t -> p k t", p=128), writes=[dat])
            for dr in range(2):
                mk.dma(of[:, dr, :, 0:w], OFW[dr, :, t0:t0 + w].rearrange("(k p) t -> p k t", p=128), writes=[dof], add=(dr > 0))
            mk.dma(gs[:, :, 0:w], GS[:, t0:t0 + w].rearrange("(k p) t -> p k t", p=128), writes=[dgs])
            I("dve", lambda e: e.tensor_tensor(bs[:, :, 0:w], of[:, 0, :, 0:w], of[:, 1, :, 0:w], ALU.add), reads=[dof], writes=[dbs])
            I("act", lambda e: e.activation(out=sqb[:, :, 0:w], in_=bs[:, :, 0:w], func=AF.Square), reads=[dbs], writes=[dsqb])
            for h in range(4):
                ps_, dps_ = mk.ps()
                MM(ps_[:, 0:w], dps_, ones[:, 1, :], sqb[:, h, 0:w], True, True, [dsqb, dones])
                I("act", lambda e: e.activation(out=rs4[:, h, 0:w], in_=ps_[:, 0:w], func=AF.Ln, bias=epsc[:, 0:1], scale=1.0),
                  reads=[dps_, deps_], writes=[drs4], add=(h > 0))
            I("act", lambda e: e.activation(out=rs4[:, :, 0:w], in_=rs4[:, :, 0:w], func=AF.Exp, scale=-0.5), reads=[drs4], writes=[drs4])
            I("pool", lambda e: e.tensor_tensor(bs[:, :, 0:w], bs[:, :, 0:w], rs4[:, :, 0:w], ALU.mult), reads=[dbs, drs4], writes=[dbs])
            I("dve", lambda e: e.scalar_tensor_tensor(bl[:, :, 0:w], bs[:, :, 0:w], ogt[:, 0:1], gs[:, :, 0:w], ALU.mult, ALU.mult),
              reads=[dbs, dog, dgs], writes=[dbl])
            for fc in range(KC):
                ps_, dps_ = mk.ps()
                for ic in range(8):
                    rhs = at[:, ic, 0:w] if ic < 4 else bl[:, ic - 4, 0:w]
                    MM(ps_[:, 0:w], dps_, Wo[:, ic, fc * 128:(fc + 1) * 128], rhs, ic == 0, ic == 7, [dWo, dat, dbl])
                I("dve", lambda e: e.scalar_tensor_tensor(xt[:, fc, 0:w], ps_[:, 0:w], MOD[:, 0, 2, fc, which:which + 1],
                                                          xt[:, fc, 0:w], ALU.mult, ALU.add),
                  reads=[dps_, dMOD, dxt], writes=[dxt])
            mk.dma(XA[:, t0:t0 + w].rearrange("(k p) t -> p k t", p=128), xt[:, :, 0:w], reads=[dxt])
        mk.barrier()
    if upto == "D1":
        return finish(nc, mk, top)

    def ffn_phase(l, src, dst, final):
        nonlocal xn_, dxn_
        with ExitStack() as es:
            Wi, dWi = load_w(es, "Wi", ffn_w_in[l], D, 2 * FH)
            Wf, dWf = load_w(es, "Wf", ffn_w_out[l], FH, D)
            xt = mk.sb(es, "xt", [128, KC, 512], F32); dxt = Dep()
            xn_ = mk.sb(es, "xn", [128, 2, 512], F32); dxn_ = [Dep(), Dep()]
            sq = mk.sb(es, "sq", [128, KC, 512], BF16); dsq = Dep()
            rst = mk.sb(es, "rst", [128, 512], F32); drst = Dep()
            hb = mk.sb(es, "hb", [128, KC, 512], BF16); dhb = Dep()
            act = mk.sb(es, "act", [128, 22, 512], BF16); dact = Dep()
            sg = mk.sb(es, "sg", [128, 2, 512], F32); dsg = [Dep(), Dep()]
            for (t0, w) in (TILES if "D2" not in skip else []):
                if final and t0 == 0:
                    continue
                which = 1 if t0 == 0 else 0
                mk.dma(xt[:, :, 0:w], src[:, t0:t0 + w].rearrange("(k p) t -> p k t", p=128), writes=[dxt])
                norm_mod(xt, dxt, hb, dhb, sq, dsq, rst, drst, l, 1, which, w)
                for hc in range(22):
                    pg, dpg = mk.ps()
                    pu, dpu = mk.ps()
                    for kc in range(KC):
                        MM(pg[:, 0:w], dpg, Wi[:, kc, hc * 128:(hc + 1) * 128], hb[:, kc, 0:w], kc == 0, kc == KC - 1, [dWi, dhb])
                    for kc in range(KC):
                        MM(pu[:, 0:w], dpu, Wi[:, kc, FH + hc * 128:FH + (hc + 1) * 128], hb[:, kc, 0:w], kc == 0, kc == KC - 1, [dWi, dhb])
                    I("act", lambda e: e.activation(out=sg[:, hc % 2, 0:w], in_=pg[:, 0:w], func=AF.Silu), reads=[dpg], writes=[dsg[hc % 2]])
                    I("dve", lambda e: e.tensor_tensor(act[:, hc, 0:w], sg[:, hc % 2, 0:w], pu[:, 0:w], ALU.mult),
                      reads=[dsg[hc % 2], dpu], writes=[dact], add=(hc > 0))
                for fc in range(KC):
                    ps_, dps_ = mk.ps()
                    for hc in range(22):
                        MM(ps_[:, 0:w], dps_, Wf[:, hc, fc * 128:(fc + 1) * 128], act[:, hc, 0:w], hc == 0, hc == 21, [dWf, dact])
                    I("dve", lambda e: e.scalar_tensor_tensor(xt[:, fc, 0:w], ps_[:, 0:w], MOD[:, l, 5, fc, which:which + 1],
                                                              xt[:, fc, 0:w], ALU.mult, ALU.add),
                      reads=[dps_, dMOD, dxt], writes=[dxt])
                if final:
                    mk.dma(dst[:, t0 - LC:t0 - LC + w].rearrange("(k p) t -> p k t", p=128), xt[:, :, 0:w], reads=[dxt])
                else:
                    mk.dma(dst[:, t0:t0 + w].rearrange("(k p) t -> p k t", p=128), xt[:, :, 0:w], reads=[dxt])
            mk.barrier()

    ffn_phase(0, XA, X1, False)
    if upto == "D2":
        return finish(nc, mk, top)

    import math
    LG = [[math.log(1.0 - 2.0 ** (-5.0 - h)) for h in range(4)]]
    LG.append(LG[0][::-1])

    with ExitStack() as es:
        W1, dW1 = load_w(es, "W1", odd_w_in, D, 6144)
        xt = mk.sb(es, "xt", [128, KC, 512], F32); dxt = Dep()
        xn_ = mk.sb(es, "xn", [128, 2, 512], F32); dxn_ = [Dep(), Dep()]
        sq = mk.sb(es, "sq", [128, KC, 512], BF16); dsq = Dep()
        rst = mk.sb(es, "rst", [128, 512], F32); drst = Dep()
        hb = mk.sb(es, "hb", [128, KC, 512], BF16); dhb = Dep()
        cs = mk.sb(es, "cs", [128, 2, 512], F32); dcs = Dep()
        gt4 = mk.sb(es, "gt4", [128, 4, 512], F32); dgt4 = Dep()
        x12_ = mk.sb(es, "x12", [128, 2, 2, 512], F32); dx12_ = [Dep(), Dep()]
        o12_ = mk.sb(es, "o12", [128, 2, 2, 512], F32); do12_ = [[Dep(), Dep()], [Dep(), Dep()]]
        tA_ = mk.sb(es, "tA", [128, 2, 512], F32); dtA_ = [Dep(), Dep()]
        tB_ = mk.sb(es, "tB", [128, 2, 512], F32); dtB_ = [Dep(), Dep()]
        qdo = mk.sb(es, "qdo", [128, 2, 2, 2, 512], BF16); dqdo = [Dep(), Dep()]
        kt2 = mk.sb(es, "kt2", [128, 2, 4, 256], BF16); dkt2 = [Dep(), Dep()]
        vs = mk.sb(es, "vs", [128, 2, 2048], BF16); dvs = [Dep(), Dep()]
        gsb = mk.sb(es, "gsb", [128, 2, 512], F32); dgsb = [Dep(), Dep()]
        for (t0, w) in (TILES if "E" not in skip else []):
            which = 1 if t0 == 0 else 0
            nst = w // 128
            mk.dma(xt[:, :, 0:w], X1[:, t0:t0 + w].rearrange("(k p) t -> p k t", p=128), writes=[dxt])
            norm_mod(xt, dxt, hb, dhb, sq, dsq, rst, drst, 1, 0, which, w)
            mk.dma(cs[:, 0, 0:w], rc1[:, t0:t0 + w], writes=[dcs])
            mk.dma(cs[:, 1, 0:w], rs1[:, t0:t0 + w], writes=[dcs], add=True)
            for h in range(4):
                mk.dma(gt4[:], gtab[h * 4:(h + 1) * 4].rearrange("i p c -> p i c"), writes=[dgt4])
                for qk in range(2):
                    x12 = x12_[:, qk]; dx12 = dx12_[qk]
                    o12 = o12_[:, qk]; do12 = do12_[qk]
                    tA = tA_[:, qk]; dtA = dtA_[qk]
                    tB = tB_[:, qk]; dtB = dtB_[qk]
                    c0 = qk * 1024 + h * 256
                    pp = []
                    for half in range(2):
                        p_, dp_ = mk.ps()
                        for kc in range(KC):
                            MM(p_[:, 0:w], dp_, W1[:, kc, c0 + half * 128:c0 + (half + 1) * 128], hb[:, kc, 0:w],
                               kc == 0, kc == KC - 1, [dW1, dhb])
                        pp.append((p_, dp_))
                    I("act", lambda e: e.copy(x12[:, 0, 0:w], pp[0][0][:, 0:w]), reads=[pp[0][1]], writes=[dx12])
                    I("act", lambda e: e.copy(x12[:, 1, 0:w], pp[1][0][:, 0:w]), reads=[pp[1][1]], writes=[dx12], add=True)
                    x1, x2 = x12[:, 0, 0:w], x12[:, 1, 0:w]
                    cosv, sinv = cs[:, 0, 0:w], cs[:, 1, 0:w]
                    I("dve", lambda e: e.tensor_tensor(tA[:, 0:w], x1, cosv, ALU.mult), reads=[dx12, dcs], writes=[dtA])
                    I("pool", lambda e: e.tensor_tensor(tB[:, 0:w], x2, sinv, ALU.mult), reads=[dx12, dcs], writes=[dtB])
                    I("dve", lambda e: e.tensor_tensor(o12[:, 0, 0:w], tA[:, 0:w], tB[:, 0:w], ALU.subtract), reads=[dtA, dtB], writes=[do12[0]])
                    I("pool", lambda e: e.tensor_tensor(tA[:, 0:w], x2, cosv, ALU.mult), reads=[dx12, dcs, do12[0]], writes=[dtA])
                    I("dve", lambda e: e.tensor_tensor(tB[:, 0:w], x1, sinv, ALU.mult), reads=[dx12, dcs, do12[0]], writes=[dtB])
                    I("pool", lambda e: e.tensor_tensor(o12[:, 1, 0:w], tA[:, 0:w], tB[:, 0:w], ALU.add), reads=[dtA, dtB], writes=[do12[1]])
                    n = 0
                    for dr in range(2):
                        g0 = 0 if (dr == 0 or w == 512) else 256
                        for half in range(2):
                            en = "dve" if n % 2 == 0 else "pool"
                            I(en, lambda e: e.tensor_tensor(qdo[:, qk, dr, half, 0:w], o12[:, half, 0:w], gt4[:, dr * 2 + qk, g0:g0 + w], ALU.mult),
                              reads=[do12[half], dgt4], writes=[dqdo[qk]], add=(n > 0))
                            n += 1
                    dst = RQD if qk == 0 else RKF
                    for dr in range(2):
                        mk.dma(dst[dr, h * 256:(h + 1) * 256, t0:t0 + w].rearrange("(f p) t -> p f t", p=128),
                               qdo[:, qk, dr, :, 0:w], reads=[dqdo[qk]])
                    if qk == 1:
                        for dr in range(2):
                            pt, dpt = mk.ps()
                            ptb = pt[:].bitcast(BF16)
                            n = 0
                            for st in range(nst):
                                for half in range(2):
                                    I("pe", lambda e: e.transpose(ptb[:, st * 256 + half * 128:st * 256 + (half + 1) * 128],
                                                                  qdo[:, 1, dr, half, st * 128:(st + 1) * 128], ident),
                                      reads=[dqdo[1], dcm], writes=[dpt], inc=(n == 2 * nst - 1), add=(n > 0))
                                    n += 1
                            I("act", lambda e: e.copy(kt2[:, dr, 0:nst, :], ptb[:, 0:nst * 256].rearrange("p (s d) -> p s d", d=256)),
                              reads=[dpt], writes=[dkt2[dr]])
                            mk.dma(RKT[dr, t0:t0 + w, h * 256:(h + 1) * 256].rearrange("(s p) d -> p s d", p=128),
                                   kt2[:, dr, 0:nst, :], reads=[dkt2[dr]])
            for st in range(nst):
                tk = slice(st * 128, (st + 1) * 128)
                for j in range(4):
                    p_, dp_ = mk.ps()
                    for kc in range(KC):
                        MM(p_[:, 0:512], dp_, hb[:, kc, tk], W1[:, kc, 2048 + j * 512:2048 + (j + 1) * 512], kc == 0, kc == KC - 1, [dhb, dW1])
                    if j % 2 == 0:
                        I("act", lambda e: e.copy(vs[:, st % 2, j * 512:(j + 1) * 512], p_[:, 0:512]), reads=[dp_], writes=[dvs[st % 2]], add=(j > 0))
                    else:
                        I("dve", lambda e: e.tensor_copy(vs[:, st % 2, j * 512:(j + 1) * 512], p_[:, 0:512]), reads=[dp_], writes=[dvs[st % 2]], add=True)
                mk.dma(RV[t0 + st * 128:t0 + (st + 1) * 128, :], vs[:, st % 2, :], reads=[dvs[st % 2]])
            if t0 != 0:
                for gc in range(16):
                    p_, dp_ = mk.ps()
                    for kc in range(KC):
                        MM(p_[:, 0:w], dp_, W1[:, kc, 4096 + gc * 128:4096 + (gc + 1) * 128], hb[:, kc, 0:w], kc == 0, kc == KC - 1, [dW1, dhb])
                    gi = gc % 2
                    I("act", lambda e: e.activation(out=gsb[:, gi, 0:w], in_=p_[:, 0:w], func=AF.Sigmoid), reads=[dp_], writes=[dgsb[gi]])
                    I("dve", lambda e: e.tensor_tensor(gsb[:, gi, 0:w], gsb[:, gi, 0:w], p_[:, 0:w], ALU.mult), reads=[dgsb[gi], dp_], writes=[dgsb[gi]])
                    mk.dma(RG[gc * 128:(gc + 1) * 128, t0:t0 + w], gsb[:, gi, 0:w], reads=[dgsb[gi]])
        mk.barrier()
    if upto == "E":
        return finish(nc, mk, top)

    with ExitStack() as es:
        SA = mk.sb(es, "RSA", [128, 8, 2, 512], F32); SR = mk.sb(es, "RSR", [128, 8, 2, 512], F32)
        Sb = mk.sb(es, "RSb", [128, 8, 2, 512], BF16)
        dSA = [Dep() for _ in range(8)]; dSR = [Dep() for _ in range(8)]; dSb = [Dep() for _ in range(8)]
        I("dve", lambda e: e.memset(SA[:], 0.0), writes=dSA)
        I("pool", lambda e: e.memset(Sb[:], 0.0), writes=dSb)
        fb = []
        for par in range(2):
            fb.append(dict(qd=mk.sb(es, "fqd", [128, 2, 512], BF16), kf=mk.sb(es, "fkf", [128, 2, 512], BF16),
                           kt=mk.sb(es, "fkt", [128, 4, 256], BF16), v=mk.sb(es, "fv", [128, 4, 512], BF16),
                           am=mk.sb(es, "fam", [128, 4, 512], BF16), ob=mk.sb(es, "fob", [128, 4, 512], F32),
                           dqd=Dep(), dkf=Dep(), dkt=Dep(), dv=Dep(), dam=Dep(), dob=Dep()))
        order = [list(range(17)), [0] + list(range(16, 0, -1))]
        cnt = 0
        for step in (range(17) if "F" not in skip else []):
            for dr in range(2):
                t0, w = TILES[order[dr][step]]
                nst = w // 128
                for h in range(4):
                    sidx = dr * 4 + h
                    B = fb[cnt % 2]
                    cnt += 1
                    mk.dma(B["kt"][:, 0:nst, :], RKT[dr, t0:t0 + w, h * 256:(h + 1) * 256].rearrange("(s p) d -> p s d", p=128), writes=[B["dkt"]])
                    mk.dma(B["v"][:, 0:nst, :], RV[t0:t0 + w, h * 512:(h + 1) * 512].rearrange("(s p) e -> p s e", p=128), writes=[B["dv"]])
                    if t0 != 0:
                        mk.dma(B["qd"][:, :, 0:w], RQD[dr, h * 256:(h + 1) * 256, t0:t0 + w].rearrange("(f p) t -> p f t", p=128), writes=[B["dqd"]])
                        mk.dma(B["kf"][:, :, 0:w], RKF[dr, h * 256:(h + 1) * 256, t0:t0 + w].rearrange("(f p) t -> p f t", p=128), writes=[B["dkf"]])
                        rng = []
                        for j in range(nst):
                            lo, hi = (j * 128, w) if dr == 0 else (0, (j + 1) * 128)
                            rng.append((lo, hi))
                            ps_, dps_ = mk.ps()
                            for half in range(2):
                                MM(ps_[:, lo:hi], dps_, B["kf"][:, half, j * 128:(j + 1) * 128], B["qd"][:, half, lo:hi],
                                   half == 0, half == 1, [B["dkf"], B["dqd"]])
                            dg = slice(j * 128, (j + 1) * 128)
                            I("dve", lambda e: e.tensor_tensor(B["am"][:, j, dg], ps_[:, dg], TRI[dr], ALU.mult),
                              reads=[dps_, dcm], writes=[B["dam"]], add=(j > 0))
                            rl, rh = ((j + 1) * 128, w) if dr == 0 else (0, j * 128)
                            if rh > rl:
                                I("dve", lambda e: e.tensor_copy(B["am"][:, j, rl:rh], ps_[:, rl:rh]), reads=[dps_], writes=[B["dam"]], add=True)
                        for ec in range(4):
                            po, dpo = mk.ps()
                            es_ = slice(ec * 128, (ec + 1) * 128)
                            MM(po[:, 0:w], dpo, Sb[:, sidx, 0, es_], B["qd"][:, 0, 0:w], True, False, [dSb[sidx], B["dqd"]])
                            MM(po[:, 0:w], dpo, Sb[:, sidx, 1, es_], B["qd"][:, 1, 0:w], False, False, [dSb[sidx], B["dqd"]])
                            for j in range(nst):
                                lo, hi = rng[j]
                                MM(po[:, lo:hi], dpo, B["v"][:, j, es_], B["am"][:, j, lo:hi], False, j == nst - 1, [B["dv"], B["dam"]])
                            if ec % 2 == 0:
                                I("act", lambda e: e.copy(B["ob"][:, ec, 0:w], po[:, 0:w]), reads=[dpo], writes=[B["dob"]], add=(ec > 0))
                            else:
                                I("dve", lambda e: e.tensor_copy(B["ob"][:, ec, 0:w], po[:, 0:w]), reads=[dpo], writes=[B["dob"]], add=True)
                        mk.dma(RO[dr, h * 512:(h + 1) * 512, t0:t0 + w].rearrange("(e p) t -> p e t", p=128), B["ob"][:, :, 0:w], reads=[B["dob"]])
                    gC = math.exp(LG[dr][h] * w)
                    for half in range(2):
                        pu_, dpu_ = mk.ps()
                        for j in range(nst):
                            MM(pu_[:, 0:512], dpu_, B["kt"][:, j, half * 128:(half + 1) * 128], B["v"][:, j, :], j == 0, j == nst - 1, [B["dkt"], B["dv"]])
                        I("dve", lambda e: e.tensor_tensor(SR[:, sidx, half, :], SA[:, sidx, half, :], pu_[:, 0:512], ALU.add),
                          reads=[dSA[sidx], dpu_], writes=[dSR[sidx]], add=(half > 0))
                    I("pool", lambda e: e.tensor_scalar(SA[:, sidx], SR[:, sidx], gC, None, ALU.mult), reads=[dSR[sidx]], writes=[dSA[sidx]])
                    I("act", lambda e: e.activation(out=Sb[:, sidx], in_=SR[:, sidx], func=AF.Identity, scale=gC), reads=[dSR[sidx]], writes=[dSb[sidx]])
        mk.barrier()
    if upto == "F":
        return finish(nc, mk, top)

    with ExitStack() as es:
        W1o, dW1o = load_w(es, "W1o", odd_w_out, 2048, D)
        xt = mk.sb(es, "xt", [128, KC, 512], F32); dxt = Dep()
        yb = mk.sb(es, "yb", [128, 16, 512], BF16); dyb = Dep()
        ro = mk.sb(es, "ro", [128, 2, 4, 512], F32); dro = Dep()
        rg = mk.sb(es, "rg", [128, 4, 512], F32); drg = Dep()
        bs = mk.sb(es, "bs", [128, 4, 512], F32); dbs = Dep()
        sqb = mk.sb(es, "sqb", [128, 4, 512], BF16); dsqb = Dep()
        rs_ = mk.sb(es, "rs_", [128, 512], F32); drs_ = Dep()
        for (t0, w) in (TILES[1:] if "G1" not in skip else []):
            mk.dma(xt[:, :, 0:w], X1[:, t0:t0 + w].rearrange("(k p) t -> p k t", p=128), writes=[dxt])
            for h in range(4):
                for dr in range(2):
                    mk.dma(ro[:, dr, :, 0:w], RO[dr, h * 512:(h + 1) * 512, t0:t0 + w].rearrange("(e p) t -> p e t", p=128), writes=[dro], add=(dr > 0))
                mk.dma(rg[:, :, 0:w], RG[h * 512:(h + 1) * 512, t0:t0 + w].rearrange("(e p) t -> p e t", p=128), writes=[drg])
                I("dve", lambda e: e.tensor_tensor(bs[:, :, 0:w], ro[:, 0, :, 0:w], ro[:, 1, :, 0:w], ALU.add), reads=[dro], writes=[dbs])
                I("act", lambda e: e.activation(out=sqb[:, :, 0:w], in_=bs[:, :, 0:w], func=AF.Square), reads=[dbs], writes=[dsqb])
                ps_, dps_ = mk.ps()
                for ec in range(4):
                    MM(ps_[:, 0:w], dps_, ones[:, 2, :], sqb[:, ec, 0:w], ec == 0, ec == 3, [dsqb, dones])
                I("act", lambda e: e.activation(out=rs_[:, 0:w], in_=ps_[:, 0:w], func=AF.Ln, bias=epsc[:, 0:1], scale=1.0), reads=[dps_, deps_], writes=[drs_])
                I("act", lambda e: e.activation(out=rs_[:, 0:w], in_=rs_[:, 0:w], func=AF.Exp, scale=-0.5), reads=[drs_], writes=[drs_])
                I("pool", lambda e: e.tensor_tensor(bs[:, :, 0:w], bs[:, :, 0:w], rs_[:, 0:w].unsqueeze(1).to_broadcast([128, 4, w]), ALU.mult),
                  reads=[dbs, drs_], writes=[dbs])
                I("dve", lambda e: e.tensor_tensor(yb[:, h * 4:(h + 1) * 4, 0:w], bs[:, :, 0:w], rg[:, :, 0:w], ALU.mult),
                  reads=[dbs, drg], writes=[dyb], add=(h > 0))
            for fc in range(KC):
                ps_, dps_ = mk.ps()
                for ic in range(16):
                    MM(ps_[:, 0:w], dps_, W1o[:, ic, fc * 128:(fc + 1) * 128], yb[:, ic, 0:w], ic == 0, ic == 15, [dW1o, dyb])
                I("dve", lambda e: e.scalar_tensor_tensor(xt[:, fc, 0:w], ps_[:, 0:w], MOD[:, 1, 2, fc, 0:1], xt[:, fc, 0:w], ALU.mult, ALU.add),
                  reads=[dps_, dMOD, dxt], writes=[dxt])
            mk.dma(XB[:, t0:t0 + w].rearrange("(k p) t -> p k t", p=128), xt[:, :, 0:w], reads=[dxt])
        mk.barrier()
    if upto == "G1":
        return finish(nc, mk, top)

    ffn_phase(1, XB, outT, True)

    return finish(nc, mk, top)


def finish(nc, mk, top):
    mk.barrier(engines=("sp",))
    top.close()
    return nc


def _const_tables():
    f32 = np.float32
    t = np.arange(L)
    row = (t // 64).astype(f32)
    col = (t % 64).astype(f32)
    inv = (f32(10000.0) ** (-np.arange(16, dtype=f32) / f32(16))).astype(f32)
    ang = np.concatenate([row[:, None] * inv, col[:, None] * inv], axis=-1).astype(f32)
    rope0 = np.zeros((T, 64), f32)
    rope0[:LC, :32] = 1.0
    rope0[LC:, :32] = np.cos(ang)
    rope0[LC:, 32:] = np.sin(ang)
    theta = (1.0 / (f32(10000.0) ** np.linspace(0.0, 1.0, 128, dtype=f32))).astype(f32)
    ang1 = (np.arange(L, dtype=f32)[:, None] * theta).astype(f32)
    rc1 = np.ones((128, T), f32)
    rs1 = np.zeros((128, T), f32)
    rc1[:, LC:] = np.cos(ang1).T
    rs1[:, LC:] = np.sin(ang1).T
    lg_fw = np.log(1.0 - 2.0 ** (-5.0 - np.arange(4, dtype=np.float64)))
    lg = [lg_fw, lg_fw[::-1]]
    gtab = np.zeros((16, 128, 512), f32)
    p = np.arange(512, dtype=np.float64)
    for h in range(4):
        for dr in range(2):
            i = p + 1.0 if dr == 0 else 512.0 - p
            gtab[(h * 2 + dr) * 2 + 0] = np.exp(lg[dr][h] * i)[None, :]
            gtab[(h * 2 + dr) * 2 + 1] = (np.exp(-lg[dr][h] * i) / 16.0)[None, :]
    cmask = np.zeros((128, 6, 512), f32)
    j = np.arange(128)[:, None]
    i = np.arange(128)[None, :]
    cmask[:, 0, :] = np.tile((j >= i).astype(f32), (1, 4))
    cmask[:, 1, :] = np.tile((j <= i).astype(f32), (1, 4))
    cmask[:, 2, :128] = (j <= i).astype(f32)
    cmask[:, 3, :128] = (j >= i).astype(f32)
    cmask[:, 4, :128] = np.eye(128, dtype=f32)
    sm = np.ones(512, f32)
    sm[::64] = 0.0
    cmask[:, 5, :] = sm[None, :]
    return rope0, rc1, rs1, gtab, cmask, lg


def _col(v):
    return np.ascontiguousarray(np.asarray(v, np.float32).reshape(-1, 128).T)


def make_in_maps(x, c, ctx, c_ctx, mod_w, mod_b, norm_g, ffn_w_in, ffn_w_out, even_w_in, even_w_out,
                 attn_qk_norm_g, attn_sink, hgrn_out_norm_g, hgrn_lb, odd_w_in, odd_w_out, cores=range(8)):
    f32 = np.float32
    A = lambda a: np.ascontiguousarray(np.asarray(a, dtype=f32))
    rope0, rc1, rs1, gtab, cmask, _ = _const_tables()
    x, c, ctx, c_ctx = A(x), A(c), A(ctx), A(c_ctx)
    mod_b, norm_g = A(mod_b), A(norm_g)
    shared = {
        "mod_w": A(mod_w),
        "modb": np.ascontiguousarray(np.stack([_col(mod_b[l]) for l in range(2)], axis=1)),
        "ngc": np.ascontiguousarray(np.stack([np.stack([_col(norm_g[l, i]) for i in range(2)], axis=1)
                                              for l in range(2)], axis=1)),
        "ffn_w_in": A(ffn_w_in), "ffn_w_out": A(ffn_w_out),
        "even_w_in": A(even_w_in)[0], "even_w_out": A(even_w_out)[0],
        "odd_w_in": A(odd_w_in)[0], "odd_w_out": A(odd_w_out)[0],
        "gqk": np.ascontiguousarray(np.broadcast_to(np.concatenate(
            [np.tile(A(attn_qk_norm_g)[0, 0], 8), np.tile(A(attn_qk_norm_g)[0, 1], 2)])[None, :], (128, 640))),
        "sink": A(attn_sink).reshape(1, 8),
        "outg": A(hgrn_out_norm_g).reshape(128, 1),
        "lbraw": np.ascontiguousarray(A(hgrn_lb).reshape(2, 4, 128).transpose(2, 0, 1)),
        "rope0": rope0, "rc1": rc1, "rs1": rs1, "gtab": gtab, "cmask": cmask,
    }
    maps = []
    for b in cores:
        m = dict(shared)
        m["xT"] = np.ascontiguousarray(np.concatenate([ctx[b], x[b]], axis=0).T)
        m["ccol"] = np.ascontiguousarray(np.stack([_col(c[b]), _col(c_ctx)], axis=2))
        maps.append(m)
    return maps


_NC_CACHE = {}


def kernel(**inputs):
    if "nc" not in _NC_CACHE:
        _NC_CACHE["nc"] = build_program()
    nc = _NC_CACHE["nc"]
    maps = make_in_maps(**inputs)
    res = run_bass_kernel_spmd(nc, maps, core_ids=list(range(8)))
    out = np.stack([np.ascontiguousarray(r["outT"].T) for r in res.results], axis=0)
    return out.astype(np.float32)
```
